# Optimizing a Trainium2 kernel written in Bass

```python
import jax, jax.numpy as jnp
from jax import lax
import numpy as np

D_MODEL = 1024
BATCH = 16
SEQ = 4096
DEPTH = 1
DEC_BATCH = 16
DEC_SEQ = 32
PAST_LEN = 4096

CHUNK = 64
N_HEADS_SB = 8
HEAD_DIM = 64
D_A = N_HEADS_SB * HEAD_DIM
D_C = 512
D_MIX = D_A + D_C
D_IN = 3 * D_A + 2 * D_C
CONV_W = 31
CONV_HIST = CONV_W - 1
D_FF = 2816
Q_BLOCK = 128
EPS = 1e-6

kernel_name = "stickbreak_conformer_hybrid_stream_step"


def _rmsnorm(x, g):
    xf = x.astype(jnp.float32)
    y = xf * lax.rsqrt(jnp.mean(xf * xf, axis=-1, keepdims=True) + EPS)
    return (y * g.astype(jnp.float32)).astype(x.dtype)


def _layernorm(x, g, b):
    xf = x.astype(jnp.float32)
    mu = jnp.mean(xf, axis=-1, keepdims=True)
    xc = xf - mu
    y = xc * lax.rsqrt(jnp.mean(xc * xc, axis=-1, keepdims=True) + EPS)
    return (y * g.astype(jnp.float32) + b.astype(jnp.float32)).astype(x.dtype)


def _sb_attention(q, k, v, q_pos, k_pos):
    B, H, Tq, hd = q.shape
    qb = min(Q_BLOCK, Tq)
    nb = Tq // qb
    scale = HEAD_DIM ** -0.5

    def block(args):
        q_blk, pos_blk = args
        z = jnp.einsum('bhqd,bhkd->bhqk', q_blk, k,
                       preferred_element_type=jnp.float32) * scale
        mask = k_pos[None, :] < pos_blk[:, None]
        log_1mb = jnp.where(mask, jax.nn.log_sigmoid(-z), 0.0)
        stick = lax.cumsum(log_1mb, axis=3, reverse=True) - log_1mb
        w = jnp.where(mask, jnp.exp(jax.nn.log_sigmoid(z) + stick), 0.0)
        return jnp.einsum('bhqk,bhkd->bhqd', w.astype(v.dtype), v)

    q_blocks = q.reshape(B, H, nb, qb, hd).transpose(2, 0, 1, 3, 4)
    pos_blocks = q_pos.reshape(nb, qb)
    out = lax.map(block, (q_blocks, pos_blocks))
    return out.transpose(1, 2, 0, 3, 4).reshape(B, H, Tq, hd)


def _causal_dwconv(u_ext, w, b):
    out = lax.conv_general_dilated(
        u_ext, w[:, None, :].astype(u_ext.dtype), window_strides=(1,), padding='VALID',
        dimension_numbers=('NWC', 'WIO', 'NWC'), feature_group_count=D_C)
    return out + b


def _layer(x, k_hist, v_hist, conv_hist, pos0,
           w_in, sb_norm_g, conv_w, conv_b, conv_ln_g, conv_ln_b, w_out,
           norm1_g, norm2_g, w_gate, w_up, w_down):
    B, T, _ = x.shape
    h = _rmsnorm(x, norm1_g)
    proj = h @ w_in
    q, k, v, a, g = jnp.split(proj, [D_A, 2 * D_A, 3 * D_A, 3 * D_A + D_C], axis=-1)

    def heads(t):
        return t.reshape(B, T, N_HEADS_SB, HEAD_DIM).transpose(0, 2, 1, 3)

    q, k, v = heads(q), heads(k), heads(v)
    k_all = k if k_hist is None else jnp.concatenate([k_hist, k], axis=2)
    v_all = v if v_hist is None else jnp.concatenate([v_hist, v], axis=2)
    q_pos = pos0 + jnp.arange(T, dtype=jnp.int32)
    k_pos = jnp.arange(k_all.shape[2], dtype=jnp.int32)
    o = _sb_attention(q, k_all, v_all, q_pos, k_pos)
    o = _rmsnorm(o, sb_norm_g[:, None, :])
    o = o.transpose(0, 2, 1, 3).reshape(B, T, D_A)

    u = a * jax.nn.sigmoid(g)
    u_ext = jnp.concatenate([conv_hist, u], axis=1)
    c = _causal_dwconv(u_ext, conv_w, conv_b)
    c = jax.nn.silu(_layernorm(c, conv_ln_g, conv_ln_b))

    x = x + jnp.concatenate([o, c], axis=-1) @ w_out
    hf = _rmsnorm(x, norm2_g)
    x = x + (jax.nn.silu(hf @ w_gate) * (hf @ w_up)) @ w_down
    return x, k, v, u_ext[:, -CONV_HIST:]


def setup_inputs(seed: int = 0) -> dict:
    key = jax.random.key(seed)
    ks = jax.random.split(key, 20)
    f32 = jnp.float32
    nrm = lambda k, shape, s: jax.random.normal(k, shape, f32) * s
    return {
        "x_prompt": nrm(ks[0], (BATCH, SEQ, D_MODEL), 1.0),
        "x_sample": nrm(ks[1], (DEC_BATCH, DEC_SEQ, D_MODEL), 1.0),
        "cache_k": nrm(ks[2], (DEPTH, DEC_BATCH, N_HEADS_SB, PAST_LEN, HEAD_DIM), 1.0),
        "cache_v": nrm(ks[3], (DEPTH, DEC_BATCH, N_HEADS_SB, PAST_LEN, HEAD_DIM), 1.0),
        "state_conv": nrm(ks[4], (DEPTH, DEC_BATCH, CONV_HIST, D_C), 0.5),
        "w_in": nrm(ks[5], (DEPTH, D_MODEL, D_IN), D_MODEL ** -0.5),
        "sb_norm_g": 1.0 + nrm(ks[6], (DEPTH, N_HEADS_SB, HEAD_DIM), 0.02),
        "conv_w": nrm(ks[7], (DEPTH, CONV_W, D_C), CONV_W ** -0.5),
        "conv_b": nrm(ks[8], (DEPTH, D_C), 0.02),
        "conv_ln_g": 1.0 + nrm(ks[9], (DEPTH, D_C), 0.02),
        "conv_ln_b": nrm(ks[10], (DEPTH, D_C), 0.02),
        "w_out": nrm(ks[11], (DEPTH, D_MIX, D_MODEL), D_MIX ** -0.5),
        "norm1_g": 1.0 + nrm(ks[12], (DEPTH, D_MODEL), 0.02),
        "norm2_g": 1.0 + nrm(ks[13], (DEPTH, D_MODEL), 0.02),
        "w_gate": nrm(ks[14], (DEPTH, D_MODEL, D_FF), D_MODEL ** -0.5),
        "w_up": nrm(ks[15], (DEPTH, D_MODEL, D_FF), D_MODEL ** -0.5),
        "w_down": nrm(ks[16], (DEPTH, D_FF, D_MODEL), D_FF ** -0.5),
        "final_g": 1.0 + nrm(ks[17], (D_MODEL,), 0.02),
    }


def reference(x_prompt, x_sample, cache_k, cache_v, state_conv,
              w_in, sb_norm_g, conv_w, conv_b, conv_ln_g, conv_ln_b, w_out,
              norm1_g, norm2_g, w_gate, w_up, w_down, final_g):
    hp, hs = x_prompt, x_sample
    kp_l, vp_l, cp_l, ks_l, vs_l, cs_l = [], [], [], [], [], []
    past = cache_k.shape[3]
    for l in range(DEPTH):
        params = (w_in[l], sb_norm_g[l], conv_w[l], conv_b[l], conv_ln_g[l], conv_ln_b[l],
                  w_out[l], norm1_g[l], norm2_g[l], w_gate[l], w_up[l], w_down[l])
        zero_hist = jnp.zeros((hp.shape[0], CONV_HIST, D_C), hp.dtype)
        hp, kp, vp, cp = _layer(hp, None, None, zero_hist, 0, *params)
        hs, ksm, vsm, csm = _layer(hs, cache_k[l], cache_v[l], state_conv[l], past, *params)
        kp_l.append(kp); vp_l.append(vp); cp_l.append(cp)
        ks_l.append(ksm); vs_l.append(vsm); cs_l.append(csm)
    y_prompt = _rmsnorm(hp, final_g)
    y_sample = _rmsnorm(hs, final_g)
    k_prompt = jnp.stack(kp_l)
    v_prompt = jnp.stack(vp_l)
    conv_prompt = jnp.stack(cp_l)
    k_sample = jnp.stack(ks_l)
    v_sample = jnp.stack(vs_l)
    conv_sample = jnp.stack(cs_l)
    return (y_prompt, y_sample, k_prompt, v_prompt, conv_prompt, k_sample, v_sample, conv_sample)
```

```python
import numpy as np
from contextlib import ExitStack
import concourse.bass as bass
import concourse.mybir as mybir
from concourse.bass_utils import run_bass_kernel_spmd

F32 = mybir.dt.float32
BF16 = mybir.dt.bfloat16
AF = mybir.ActivationFunctionType
ALU = mybir.AluOpType
AX = mybir.AxisListType

T = 4096
D = 1024
NH = 8
HD = 64
DIN = 2560
DFF = 2816
NF = DFF // 128
CW = 31
CH = 30
EPS = 1e-6
GT = 256
NB1 = GT // 128
ST = 32
NEG = -30000.0
SAME_SYNC = True
PARANOID = False
DEBUG = False
DBG = {}
ARENA_BYTES = 212736


class Prog:
    def __init__(self, nc, stack):
        self.nc, self.stack = nc, stack
        self.engs = ['pe', 'act', 'dve', 'pool', 'sp']
        self.q = {e: [] for e in self.engs}
        self.sems = []
        self.cur = {}
        self.cnt = {}
        for e in self.engs:
            self._newsem(e)
        self.lastw = {}
        self.rd = {}
        self.waited = {e: {} for e in self.engs}
        self.dsem = {}

    def _alloc(self, name):
        h = self.stack.enter_context(self.nc.semaphore(name))
        self.sems.append(h)
        return len(self.sems) - 1

    def _newsem(self, e):
        self.cur[e] = self._alloc(f"s{e}{len(self.sems)}")
        self.cnt[e] = 0

    def _need(self, eng, tok, waits, war=False, force=False):
        s, v, te = tok
        if te == eng and not force:
            if eng == 'pe' or war or not SAME_SYNC:
                return
        if self.waited[eng].get(s, 0) >= v:
            return
        self.waited[eng][s] = v
        for i, (s2, v2) in enumerate(waits):
            if s2 == s:
                waits[i] = (s, max(v, v2))
                return
        waits.append((s, v))

    def _deps(self, eng, reads, writes):
        waits = []
        for r in reads:
            t = self.lastw.get(r)
            if t:
                self._need(eng, t, waits)
        for w in writes:
            t = self.lastw.get(w)
            if t:
                self._need(eng, t, waits)
            for t in self.rd.get(w, {}).values():
                self._need(eng, t, waits, war=True)
        return waits

    def _commit(self, tok, reads, writes):
        for r in reads:
            self.rd.setdefault(r, {})[tok[0]] = tok
        for w in writes:
            self.lastw[w] = tok
            self.rd[w] = {}

    def _paranoid(self, eng, waits):
        for e in self.engs:
            if self.cnt[e] > 0:
                self._need(eng, (self.cur[e], self.cnt[e], e), waits, force=(e != 'pe' or eng != 'pe'))
        for d in self.dsem.values():
            self._need(eng, (d[0], d[1], None), waits)

    def op(self, eng, fn, reads=(), writes=()):
        waits = self._deps(eng, reads, writes)
        if PARANOID:
            self._paranoid(eng, waits)
        if self.cnt[eng] >= 60000:
            self._newsem(eng)
        self.cnt[eng] += 1
        tok = (self.cur[eng], self.cnt[eng], eng)
        self._commit(tok, reads, writes)
        self.q[eng].append((waits, fn, self.cur[eng], 1))

    def dma(self, eng, fn, reads=(), writes=(), key=None):
        waits = self._deps(eng, reads, writes)
        if PARANOID:
            self._paranoid(eng, waits)
        if key not in self.dsem or self.dsem[key][1] >= 60000:
            self.dsem[key] = [self._alloc(f"d{len(self.sems)}"), 0]
        d = self.dsem[key]
        d[1] += 16
        tok = (d[0], d[1], None)
        self._commit(tok, reads, writes)
        self.q[eng].append((waits, fn, d[0], 16))

    def barrier(self):
        toks = [(self.cur[e], self.cnt[e], e) for e in self.engs if self.cnt[e] > 0]
        toks += [(d[0], d[1], None) for d in self.dsem.values()]
        for e in self.engs:
            waits = []
            for t in toks:
                self._need(e, t, waits, force=True)
            self.q[e].append((waits, None, None, 0))
        self.lastw.clear()
        self.rd.clear()

    def emit(self, block):
        engmap = {'pe': block.tensor, 'act': block.scalar, 'dve': block.vector,
                  'pool': block.gpsimd, 'sp': block.sync}
        for e in self.engs:
            items = self.q[e]

            def body(eng, items=items):
                for waits, fn, sem, inc in items:
                    for s, v in waits:
                        eng.wait_ge(self.sems[s], v)
                    if fn is not None:
                        fn(eng).then_inc(self.sems[sem], inc)
            engmap[e](body)


class Arena:
    def __init__(self, nc, stack, nbytes):
        self.t = stack.enter_context(nc.sbuf_tensor("arena", [128, nbytes // 2], BF16))
        self.size = nbytes
        self.top = 0

    def alloc(self, nbytes):
        off = self.top
        self.top += (nbytes + 63) // 64 * 64
        assert self.top <= self.size, f"SBUF arena overflow {self.top} > {self.size}"
        return off

    def bf(self, n):
        off = self.alloc(n * 2)
        return self.t[:, off // 2: off // 2 + n]

    def f32(self, n):
        off = self.alloc(n * 4)
        return self.t[:, off // 2: off // 2 + 2 * n].bitcast(F32)


def r3(ap, b):
    return ap.rearrange("p (a b) -> p a b", b=b)


def build_program(nc):
    stack = ExitStack()
    P = Prog(nc, stack)
    A = Arena(nc, stack, ARENA_BYTES)

    def din(name, shape):
        return nc.dram_tensor(name, list(shape), F32, kind="ExternalInput").ap()

    def dout(name, shape):
        return nc.dram_tensor(name, list(shape), F32, kind="ExternalOutput").ap()

    xp = din("xp", [2 * T, D]); xs = din("xs", [2 * ST, D])
    ck = din("ck", [2, NH, T, HD]); cv = din("cv", [2, NH, T, HD])
    sc = din("sc", [2 * CH, 512])
    w_in = din("w_in", [D, DIN]); sbg_d = din("sbg", [512]); cw_d = din("convw", [CW, 512])
    cb_d = din("convb", [512]); lg_d = din("lng", [512]); lb_d = din("lnb", [512])
    w_out = din("w_out", [D, D]); g1_d = din("g1", [D]); g2_d = din("g2", [D])
    w_gate = din("w_gate", [D, DFF]); w_up = din("w_up", [D, DFF]); w_down = din("w_down", [DFF, D])
    fg_d = din("fg", [D]); cst_d = din("cst", [128, 384])
    yp = dout("yp", [2 * T, D]); ys = dout("ys", [2 * ST, D])
    kp = dout("kp", [2, NH, T, HD]); vp = dout("vp", [2, NH, T, HD]); cp = dout("cp", [2, CH, 512])
    ks = dout("ks", [2, NH, ST, HD]); vs = dout("vs", [2, NH, ST, HD]); cs = dout("cs", [2, CH, 512])
    dk = dict(kind="ExternalOutput") if DEBUG else {}
    x2d = nc.dram_tensor("x2d", [2 * T + 2 * ST, D], F32, **dk).ap()
    qTd = nc.dram_tensor("qTd", [2, 4, 128, T], BF16, **dk).ap()
    cTd = nc.dram_tensor("cTd", [2, 4, 128, T], BF16, **dk).ap()

    ps = [stack.enter_context(nc.psum_tensor(f"ps{i}", [128, 512], F32)) for i in range(8)]

    def psf(i):
        return ps[i][:, :]

    def psb(i):
        return ps[i][:, :].bitcast(BF16)

    dbg_n = [0]

    def dump(name, ap, shape, reads, dt=F32):
        if not DEBUG or name in DBG.get('_done', set()):
            return
        DBG.setdefault('_done', set()).add(name)
        t = nc.dram_tensor("dbg_" + name, list(shape), dt, kind="ExternalOutput").ap()
        P.dma('sp', lambda e: e.dma_start(out=t, in_=ap), reads=reads, key=('dbg', name))

    identF = A.f32(128); identB = A.bf(128); negB = A.bf(128); onesB = A.bf(128)
    epsT = A.f32(1)
    pv = A.f32(160)
    pvA = A.f32(128); pvC = A.f32(128)
    g1 = pv[:, 0:8]; g2 = pv[:, 8:16]; sbg = pv[:, 16:20]; cbv = pv[:, 20:24]
    lgv = pv[:, 24:28]; lbv = pv[:, 28:32]
    sm = A.f32(64)
    mh = A.f32(8)
    base_top = A.top

    P.dma('sp', lambda e: e.dma_start(out=identF, in_=cst_d[:, 0:128]), writes=['identF'], key='k_identF')
    P.dma('pool', lambda e: e.dma_start(out=identB, in_=cst_d[:, 0:128]), writes=['identB'], key='k_identB')
    P.dma('pool', lambda e: e.dma_start(out=negB, in_=cst_d[:, 128:256]), writes=['negB'], key='k_negB')
    P.dma('pool', lambda e: e.dma_start(out=onesB, in_=cst_d[:, 256:384]), writes=['onesB'], key='k_onesB')
    P.op('pool', lambda e: e.memset(epsT, EPS), writes=['epsT'])
    P.op('pool', lambda e: e.memset(mh, -0.5), writes=['mh'])

    def rstd_pool(dst, src, scale, w, rd, wr):
        P.op('pool', lambda e: e.tensor_scalar(out=dst, in0=src, scalar1=scale, scalar2=EPS, op0=ALU.mult, op1=ALU.add),
             reads=rd, writes=wr)
        P.op('pool', lambda e: e.tensor_tensor(out=dst, in0=dst, in1=mh[0:dst.shape[0], 0:w], op=ALU.pow),
             reads=wr + ['mh'], writes=wr)
    for i, v in enumerate([g1_d, g2_d]):
        P.dma('sp', lambda e, i=i, v=v: e.dma_start(out=pvA[8 * i:8 * i + 8, :], in_=v.rearrange("(k p) -> k p", p=128)),
              writes=['pvA'], key='k_pvA')
    for i, v in enumerate([sbg_d, cb_d, lg_d, lb_d]):
        P.dma('sp', lambda e, i=i, v=v: e.dma_start(out=pvA[16 + 4 * i:20 + 4 * i, :], in_=v.rearrange("(k p) -> k p", p=128)),
              writes=['pvA'], key='k_pvA')
    P.dma('sp', lambda e: e.dma_start(out=pvC[0:124, :], in_=cw_d.rearrange("k (g p) -> (k g) p", p=128)),
          writes=['pvC'], key='k_pvC')
    P.op('pe', lambda e: e.transpose(out=psf(0)[:, 0:32], in_=pvA[0:32, :], identity=identF[0:32, 0:32]),
         reads=['pvA', 'identF'], writes=[('ps', 0)])
    P.op('pe', lambda e: e.transpose(out=psf(0)[:, 32:156], in_=pvC[0:124, :], identity=identF[0:124, 0:124]),
         reads=['pvC', 'identF'], writes=[('ps', 0)])
    P.op('dve', lambda e: e.tensor_copy(out=pv[:, 0:156], in_=psf(0)[:, 0:156]), reads=[('ps', 0)], writes=['pv'])

    kT = r3(A.bf(4 * T), T)
    vv = r3(A.bf(32 * 512), 512)
    win_off = A.top
    win = r3(A.bf(8 * DIN), DIN)
    convD = A.bf(124 * 128)
    wout = r3(A.bf(8 * D), D)
    phase_top = A.top

    wst = [A.f32(DIN) for _ in range(4)]
    A.top = phase_top
    widx = [0]

    def load_cast(dst, src, n, scale_ap, stgs, tag):
        i = widx[0]; widx[0] += 1
        st_ = stgs[i % len(stgs)]
        sk = (tag, i % len(stgs))
        P.dma('sp', lambda e: e.dma_start(out=st_[:, 0:n], in_=src), writes=[sk], key=sk)
        if i % 2 == 0:
            if scale_ap is None:
                P.op('act', lambda e: e.activation(out=dst, in_=st_[:, 0:n], func=AF.Identity), reads=[sk], writes=[])
            else:
                P.op('act', lambda e: e.activation(out=dst, in_=st_[:, 0:n], func=AF.Identity, scale=scale_ap),
                     reads=[sk, 'pv'], writes=[])
        else:
            if scale_ap is None:
                P.op('dve', lambda e: e.tensor_copy(out=dst, in_=st_[:, 0:n]), reads=[sk], writes=[])
            else:
                P.op('dve', lambda e: e.tensor_scalar(out=dst, in0=st_[:, 0:n], scalar1=scale_ap, scalar2=None, op0=ALU.mult),
                     reads=[sk, 'pv'], writes=[])

    for kc in range(8):
        load_cast(win[:, kc, :], w_in[kc * 128:(kc + 1) * 128, :], DIN, g1[:, kc:kc + 1], wst, 'wst')
    for kc in range(8):
        load_cast(wout[:, kc, :], w_out[kc * 128:(kc + 1) * 128, :], D, sbg[:, kc:kc + 1] if kc < 4 else None, wst, 'wst')
    for k in range(CW):
        for g in range(4):
            idx = k * 4 + g
            eng = 'dve' if idx % 2 == 0 else 'pool'
            P.op(eng, lambda e, idx=idx: e.tensor_scalar(out=convD[:, idx * 128:(idx + 1) * 128], in0=identF,
                                                         scalar1=pv[:, 32 + idx:33 + idx], scalar2=None, op0=ALU.mult),
                 reads=['pv', 'identF'], writes=['convD'])
    P.barrier()

    def p1_alloc(nx=2):
        B = {}
        B['xblk'] = [A.f32(D) for _ in range(nx)]
        B['xn'] = [A.bf(D) for _ in range(nx)]
        B['hT'] = [r3(A.bf(8 * GT), GT) for _ in range(2)]
        B['qTs'] = r3(A.bf(4 * GT), GT)
        B['cTs'] = r3(A.bf(4 * GT), GT)
        B['kst'] = [A.f32(512) for _ in range(2)]
        B['vst'] = [A.f32(512) for _ in range(2)]
        B['sig'] = [A.f32(GT) for _ in range(2)]
        B['ub'] = [r3(A.bf(4 * (CH + GT + 2)), CH + GT + 2) for _ in range(2)]
        B['cfp'] = r3(A.f32(4 * GT), GT)
        B['cb16'] = r3(A.bf(4 * GT), GT)
        B['csq'] = r3(A.bf(4 * GT), GT)
        B['mm'] = A.f32(GT); B['rs2'] = A.f32(GT); B['msq'] = A.f32(GT)
        B['uf32'] = r3(A.f32(4 * 32), 32)
        return B

    gen_rot = [0]

    def gbank():
        gen_rot[0] = (gen_rot[0] + 1) % 4
        return 1 + gen_rot[0]

    conv_rot = [0]

    def norm_pre(B, slot, src_ap, rows, xkey):
        xb = B['xblk'][slot]; xn = B['xn'][slot]
        P.dma('pool', lambda e: e.dma_start(out=xb[0:rows, :], in_=src_ap), writes=[('xblk', slot)], key=(xkey, slot))
        P.op('act', lambda e: e.activation(out=xn[0:rows, :], in_=xb[0:rows, :], func=AF.Square,
                                           accum_out=sm[0:rows, slot:slot + 1]),
             reads=[('xblk', slot)], writes=[('xn', slot), ('ss', slot)])
        P.op('act', lambda e: e.activation(out=sm[0:rows, 2 + slot:3 + slot], in_=sm[0:rows, slot:slot + 1], func=AF.Sqrt,
                                           scale=1.0 / D, bias=epsT[0:rows, :]),
             reads=[('ss', slot), 'epsT'], writes=[('rs', slot)])
        P.op('dve', lambda e: e.reciprocal(out=sm[0:rows, 2 + slot:3 + slot], in_=sm[0:rows, 2 + slot:3 + slot]),
             reads=[('rs', slot)], writes=[('rs', slot)])
        P.op('dve', lambda e: e.tensor_scalar(out=xn[0:rows, :], in0=xb[0:rows, :], scalar1=sm[0:rows, 2 + slot:3 + slot],
                                              scalar2=None, op0=ALU.mult),
             reads=[('xblk', slot), ('rs', slot)], writes=[('xn', slot)])

    def norm_post(B, slot, rows, hs, col0):
        xn = B['xn'][slot]; hT = B['hT'][hs]
        tp = r3(psb(0), 128)
        for kc in range(8):
            P.op('pe', lambda e, kc=kc: e.transpose(out=tp[:, kc, 0:rows], in_=xn[0:rows, kc * 128:(kc + 1) * 128],
                                                    identity=identB[0:rows, 0:rows]),
                 reads=[('xn', slot), 'identB'], writes=[('ps', 0)])
        P.op('act', lambda e: e.activation(out=hT[:, :, col0:col0 + rows], in_=tp[:, :, 0:rows], func=AF.Copy),
             reads=[('ps', 0)], writes=[('hT', hs)])

    def norm_block(B, slot, src_ap, rows, hs, col0, xkey):
        norm_pre(B, slot, src_ap, rows, xkey)
        norm_post(B, slot, rows, hs, col0)

    def mm_group(bank, n, lhs_fn, rhs_fn, nk, reads, m=128):
        for kc in range(nk):
            lhs = lhs_fn(kc); rhs = rhs_fn(kc)
            P.op('pe', lambda e, kc=kc, lhs=lhs, rhs=rhs: e.matmul(psf(bank)[0:m, 0:n], lhsT=lhs, rhs=rhs,
                                                                   start=(kc == 0), stop=(kc == nk - 1)),
                 reads=reads, writes=[('ps', bank)])

    WIN_ALL = [('win', kc) for kc in range(8)]
    evac_rot = [0]

    def evac(out_ap, in_ap, reads, writes):
        evac_rot[0] ^= 1
        if evac_rot[0]:
            P.op('act', lambda e: e.activation(out=out_ap, in_=in_ap, func=AF.Copy), reads=reads, writes=writes)
        else:
            P.op('dve', lambda e: e.tensor_copy(out=out_ap, in_=in_ap), reads=reads, writes=writes)

    def p1_feature(B, hs, n, us, segs, u32bufs, split=False):
        hT = B['hT'][hs]; ub = B['ub'][us]
        for cg in range(4):
            ab, gb_ = (3, 4) if cg % 2 == 0 else (1, 2)
            mm_group(ab, n, lambda kc: win[:, kc, 1536 + cg * 128:1536 + (cg + 1) * 128], lambda kc: hT[:, kc, 0:n], 8,
                     [('hT', hs)] + WIN_ALL)
            mm_group(gb_, n, lambda kc: win[:, kc, 2048 + cg * 128:2048 + (cg + 1) * 128], lambda kc: hT[:, kc, 0:n], 8,
                     [('hT', hs)] + WIN_ALL)
            sg = B['sig'][cg % 2]
            P.op('act', lambda e, sg=sg, gb_=gb_: e.activation(out=sg[:, 0:n], in_=psf(gb_)[:, 0:n], func=AF.Sigmoid),
                 reads=[('ps', gb_)], writes=[('sig', cg % 2)])
            for (c0, ln, u0) in segs:
                P.op('dve', lambda e, sg=sg, c0=c0, ln=ln, u0=u0, cg=cg, ab=ab: e.tensor_tensor(
                    out=ub[:, cg, u0 + CH:u0 + CH + ln], in0=psf(ab)[:, c0:c0 + ln], in1=sg[:, c0:c0 + ln], op=ALU.mult),
                    reads=[('ps', ab), ('sig', cg % 2)], writes=[('ubm', us)])
                if u32bufs is not None:
                    ubuf = u32bufs[segs.index((c0, ln, u0))]
                    P.op('dve', lambda e, sg=sg, c0=c0, ln=ln, cg=cg, ubuf=ubuf, ab=ab: e.tensor_tensor(
                        out=ubuf[:, cg, 0:CH], in0=psf(ab)[:, c0 + ln - CH:c0 + ln], in1=sg[:, c0 + ln - CH:c0 + ln],
                        op=ALU.mult), reads=[('ps', ab), ('sig', cg % 2)], writes=[('uf32', c0)])
        for cg in range(4):
            conv_rot[0] ^= 1
            bank = 5 + conv_rot[0]
            for (c0, ln, u0) in segs:
                for k in range(CW):
                    idx = k * 4 + cg
                    P.op('pe', lambda e, idx=idx, k=k, c0=c0, ln=ln, u0=u0, cg=cg, bank=bank: e.matmul(
                        psf(bank)[:, c0:c0 + ln], lhsT=convD[:, idx * 128:(idx + 1) * 128], rhs=ub[:, cg, u0 + k:u0 + k + ln],
                        start=(k == 0), stop=(k == CW - 1)),
                        reads=[('ubm', us), ('ubh', us), 'convD'], writes=[('ps', bank)])
            P.op('act', lambda e, cg=cg, bank=bank: e.activation(out=B['cfp'][:, cg, 0:n], in_=psf(bank)[:, 0:n], func=AF.Identity,
                                                                 bias=cbv[:, cg:cg + 1]),
                 reads=[('ps', bank), 'pv'], writes=[('cfp', cg)])
            P.op('act', lambda e, cg=cg, bank=bank: e.activation(out=B['csq'][:, cg, 0:n], in_=psf(bank)[:, 0:n], func=AF.Square,
                                                                 bias=cbv[:, cg:cg + 1]),
                 reads=[('ps', bank), 'pv'], writes=[('csq', cg)])
            P.op('act', lambda e, cg=cg, bank=bank: e.activation(out=B['cb16'][:, cg, 0:n], in_=psf(bank)[:, 0:n], func=AF.Identity,
                                                                 bias=cbv[:, cg:cg + 1]),
                 reads=[('ps', bank), 'pv'], writes=[('cb16', cg)])
        if split:
            return lambda: p1_feature_tail(B, n, True)
        p1_feature_tail(B, n)

    def p1_feature_tail(B, n, split=False):
        mm_ = None
        dump('craw', B['cfp'][:, :, 0:n].rearrange("p a b -> p (a b)") if False else B['cfp'][:, 0, 0:n], [128, n], [('cfp', 0)])
        dump('convD', convD[:, 0:512], [128, 512], ['convD'], BF16)
        dump('pv', pv, [128, 160], ['pv'])
        for cg in range(4):
            P.op('pe', lambda e, cg=cg: e.matmul(psf(7)[:, 0:n], lhsT=onesB, rhs=B['cb16'][:, cg, 0:n],
                                                 start=(cg == 0), stop=(cg == 3)),
                 reads=[('cb16', cg), 'onesB'], writes=[('ps', 7)])
        for cg in range(4):
            P.op('pe', lambda e, cg=cg: e.matmul(psf(7)[:, GT:GT + n], lhsT=onesB, rhs=B['csq'][:, cg, 0:n],
                                                 start=(cg == 0), stop=(cg == 3)),
                 reads=[('csq', cg), 'onesB'], writes=[('ps', 7)])
        mm_, rs2, msq = B['mm'], B['rs2'], B['msq']
        P.op('dve', lambda e: e.tensor_scalar(out=mm_[:, 0:n], in0=psf(7)[:, 0:n], scalar1=1.0 / 512, scalar2=None, op0=ALU.mult),
             reads=[('ps', 7)], writes=['mm'])
        P.op('dve', lambda e: e.tensor_tensor(out=msq[:, 0:n], in0=mm_[:, 0:n], in1=mm_[:, 0:n], op=ALU.mult),
             reads=['mm'], writes=['msq'])
        P.op('dve', lambda e: e.scalar_tensor_tensor(out=rs2[:, 0:n], in0=psf(7)[:, GT:GT + n], scalar=1.0 / 512, in1=msq[:, 0:n],
                                                     op0=ALU.mult, op1=ALU.subtract),
             reads=[('ps', 7), 'msq'], writes=['rs2'])
        P.op('act', lambda e: e.activation(out=rs2[:, 0:n], in_=rs2[:, 0:n], func=AF.Sqrt, bias=epsT, scale=1.0),
             reads=['rs2', 'epsT'], writes=['rs2'])
        P.op('dve', lambda e: e.reciprocal(out=rs2[:, 0:n], in_=rs2[:, 0:n]), reads=['rs2'], writes=['rs2'])
        if split:
            return lambda: p1_feature_finish(B, n)
        p1_feature_finish(B, n)

    def p1_feature_finish(B, n):
        mm_, rs2 = B['mm'], B['rs2']
        for cg in range(4):
            sgt = B['sig'][cg % 2]
            P.op('dve', lambda e, cg=cg: e.tensor_tensor(out=B['cfp'][:, cg, 0:n], in0=B['cfp'][:, cg, 0:n], in1=mm_[:, 0:n],
                                                         op=ALU.subtract), reads=[('cfp', cg), 'mm'], writes=[('cfp', cg)])
            P.op('dve', lambda e, cg=cg: e.scalar_tensor_tensor(out=B['cfp'][:, cg, 0:n], in0=B['cfp'][:, cg, 0:n],
                                                                scalar=lgv[:, cg:cg + 1], in1=rs2[:, 0:n],
                                                                op0=ALU.mult, op1=ALU.mult),
                 reads=[('cfp', cg), 'rs2', 'pv'], writes=[('cfp', cg)])
            P.op('act', lambda e, cg=cg, sgt=sgt: e.activation(out=sgt[:, 0:n], in_=B['cfp'][:, cg, 0:n], func=AF.Sigmoid,
                                                               bias=lbv[:, cg:cg + 1]),
                 reads=[('cfp', cg), 'pv'], writes=[('sig', cg % 2)])
            P.op('dve', lambda e, cg=cg, sgt=sgt: e.scalar_tensor_tensor(out=B['cTs'][:, cg, 0:n], in0=B['cfp'][:, cg, 0:n],
                                                                         scalar=lbv[:, cg:cg + 1], in1=sgt[:, 0:n],
                                                                         op0=ALU.add, op1=ALU.mult),
                 reads=[('cfp', cg), ('sig', cg % 2), 'pv'], writes=['cTs'])

    def conv_out(B, dst_ap, key, ubuf=None):
        ubuf = B['uf32'] if ubuf is None else ubuf
        for cg in range(4):
            P.op('pe', lambda e, cg=cg: e.transpose(out=psf(0)[0:CH, cg * 128:(cg + 1) * 128], in_=ubuf[:, cg, 0:CH],
                                                    identity=identF),
                 reads=[('uf32', key), 'identF'], writes=[('ps', 0)])
        P.op('act', lambda e: e.activation(out=B['kst'][0][0:CH, :], in_=psf(0)[0:CH, :], func=AF.Copy),
             reads=[('ps', 0)], writes=[('kst', 0)])
        P.dma('sp', lambda e: e.dma_start(out=dst_ap, in_=B['kst'][0][0:CH, :]), reads=[('kst', 0)], key=('kst', 0))

    def p1_prompt(B, b):
        P.op('pool', lambda e: e.memset(B['ub'][0][:, :, 0:CH], 0.0), writes=[('ubh', 0)])
        nsb = T // GT

        def norms_pre(sb):
            for j in range(NB1):
                blk = sb * NB1 + j
                r0 = b * T + blk * 128
                norm_pre(B, blk % 2, xp[r0:r0 + 128, :], 128, 'x')

        def norms_post(sb):
            for j in range(NB1):
                blk = sb * NB1 + j
                norm_post(B, blk % 2, 128, sb % 2, j * 128)

        norms_pre(0)
        norms_post(0)
        for sb in range(nsb):
            hs = sb % 2; us = sb % 2; t0 = sb * GT
            hT = B['hT'][hs]
            last = (sb == nsb - 1)
            tail = p1_feature(B, hs, GT, us, [(0, GT, 0)], [B['uf32']] if last else None, split=True)
            tail = tail()
            if not last:
                norms_pre(sb + 1)
            for fg in range(4):
                bank = gbank()
                mm_group(bank, GT, lambda kc: win[:, kc, fg * 128:(fg + 1) * 128], lambda kc: hT[:, kc, :], 8,
                         [('hT', hs)] + WIN_ALL)
                evac(B['qTs'][:, fg, :], psf(bank)[:, 0:GT], [('ps', bank)], ['qTs'])
            P.dma('sp', lambda e, t0=t0: e.dma_start(out=qTd[b].rearrange("g p t -> p g t")[:, :, t0:t0 + GT], in_=B['qTs']),
                  reads=['qTs'], writes=[('qTd', sb)], key='qTs')
            for fg in range(4):
                bank = gbank()
                mm_group(bank, GT, lambda kc: win[:, kc, 512 + fg * 128:512 + (fg + 1) * 128], lambda kc: hT[:, kc, :], 8,
                         [('hT', hs)] + WIN_ALL)
                evac(kT[:, fg, t0:t0 + GT], psf(bank)[:, 0:GT], [('ps', bank)], [('kT', sb)])
            for j in range(NB1):
                blk = sb * NB1 + j; sl = blk % 2; tt = blk * 128
                bank = gbank()
                mm_group(bank, 512, lambda kc: hT[:, kc, j * 128:(j + 1) * 128], lambda kc: win[:, kc, 512:1024], 8,
                         [('hT', hs)] + WIN_ALL)
                evac(B['kst'][sl], psf(bank), [('ps', bank)], [('kst', sl)])
                P.dma('sp', lambda e, sl=sl, tt=tt: e.dma_start(
                    out=kp[b, :, tt:tt + 128, :].rearrange("h t d -> t h d"), in_=r3(B['kst'][sl], HD)),
                    reads=[('kst', sl)], key=('kst', sl))
                bank = gbank()
                mm_group(bank, 512, lambda kc: hT[:, kc, j * 128:(j + 1) * 128], lambda kc: win[:, kc, 1024:1536], 8,
                         [('hT', hs)] + WIN_ALL)
                evac(B['vst'][sl], psf(bank), [('ps', bank)], [('vst', sl)])
                P.op('pool', lambda e, sl=sl, blk=blk: e.tensor_copy(out=vv[:, blk, :], in_=B['vst'][sl]),
                     reads=[('vst', sl)], writes=[('vv', blk)])
                P.dma('sp', lambda e, sl=sl, tt=tt: e.dma_start(
                    out=vp[b, :, tt:tt + 128, :].rearrange("h t d -> t h d"), in_=r3(B['vst'][sl], HD)),
                    reads=[('vst', sl)], key=('vst', sl))
            if not last:
                norms_post(sb + 1)
            tail()
            P.dma('sp', lambda e, t0=t0: e.dma_start(out=cTd[b].rearrange("g p t -> p g t")[:, :, t0:t0 + GT], in_=B['cTs']),
                  reads=['cTs'], writes=[('cTd', sb)], key='cTs')
            if not last:
                P.op('pool', lambda e, us=us: e.tensor_copy(out=B['ub'][1 - us][:, :, 0:CH], in_=B['ub'][us][:, :, GT:GT + CH]),
                     reads=[('ubm', us)], writes=[('ubh', 1 - us)])
            else:
                conv_out(B, cp[b], 0)

    def p2_alloc():
        B = {}
        B['qe'] = [r3(A.bf(4 * 128), 128) for _ in range(2)]
        B['qo'] = [r3(A.bf(4 * 128), 128) for _ in range(2)]
        for i in range(2):
            P.op('pool', lambda e, i=i: e.memset(B['qe'][i][64:128, :, :], 0.0), writes=[('qblk', i)])
            P.op('pool', lambda e, i=i: e.memset(B['qo'][i][0:64, :, :], 0.0), writes=[('qblk', i)])
        B['cblk'] = [r3(A.bf(4 * 128), 128) for _ in range(2)]
        B['xres'] = [A.f32(D) for _ in range(2)]
        B['pbuf'] = [A.f32(514) for _ in range(3)]
        B['Cbuf'] = [A.f32(514) for _ in range(4)]
        B['wbuf'] = [A.bf(512) for _ in range(5)]
        B['wT'] = [r3(A.bf(512), 128) for _ in range(4)]
        B['osb'] = A.f32(512)
        B['onb'] = A.bf(512); B['sq'] = B['onb']; B['onT'] = r3(A.bf(512), 128)
        B['x2s'] = [A.f32(D)] * 2
        for i in range(3):
            P.op('pool', lambda e, i=i: e.memset(B['pbuf'][i][:, 512:513], 1.0), writes=[('pbuf', i)])
        P.op('dve', lambda e: e.memset(psf(6)[:, 0:16], 0.0), writes=['pszero'])
        return B

    class AttnPipe:
        def __init__(self, B):
            self.B = B
            self.items = []
            self.n = 0
            self.deferred = {}

        def add(self, item):
            self.items.append(item)

        def run(self):
            B = self.B
            n = len(self.items)
            for step in range(n + 12):
                if step < n:
                    self.s01(step)
                if 0 <= step - 1 < n:
                    self.s23(step - 1)
                if 0 <= step - 4 < n:
                    self.s45(step - 4)
                if 0 <= step - 6 < n:
                    self.s6(step - 6)
                for fn in self.deferred.pop(step, []):
                    fn()
            assert not self.deferred

        def defer(self, step, fn):
            self.deferred.setdefault(step, []).append(fn)

        def s01(self, i):
            it = self.items[i]
            if it.get('pre'):
                it['pre']()
            R, W = it['R'], it['W']
            zb = 1 + (i % 2)
            sl = i % 3
            qT = it['qT']; kTa = it['kT']
            rd = it['rd']
            if it['masked']:
                mw = it['mw']
                if W > mw:
                    P.op('pe', lambda e: e.matmul(psf(zb)[0:R, 0:W - mw], lhsT=qT, rhs=kTa[:, 0:W - mw], start=True, stop=True),
                         reads=rd, writes=[('ps', zb)])
                P.op('pe', lambda e: e.matmul(psf(zb)[0:R, W - mw:W], lhsT=qT, rhs=kTa[:, W - mw:W], start=True, stop=False),
                     reads=rd, writes=[('ps', zb)])
                P.op('pe', lambda e: e.matmul(psf(zb)[0:R, W - mw:W], lhsT=identB[0:R, 0:R], rhs=negB[0:R, 0:mw],
                                              start=False, stop=True),
                     reads=['identB', 'negB'], writes=[('ps', zb)])
            else:
                P.op('pe', lambda e: e.matmul(psf(zb)[0:R, 0:W], lhsT=qT, rhs=kTa, start=True, stop=True),
                     reads=rd, writes=[('ps', zb)])
            pb = B_ = self.B['pbuf'][sl]
            P.op('act', lambda e: e.activation(out=pb[0:R, 512 - W:512], in_=psf(zb)[0:R, 0:W], func=AF.Sigmoid, scale=-0.125),
                 reads=[('ps', zb)], writes=[('pbuf', sl)])

        def s23(self, i):
            it = self.items[i]
            R, W = it['R'], it['W']
            sl = i % 3
            s4 = i % 4
            s5 = i % 5
            pb = self.B['pbuf'][sl]; cb = self.B['Cbuf'][s4]; wb = self.B['wbuf'][s5]
            if it['first']:
                init = 1.0
                rds = [('pbuf', sl), 'pszero']
            else:
                pit = self.items[i - 1]
                psl = (i - 1) % 4
                init = self.B['Cbuf'][psl][0:R, 512 - pit['W']:513 - pit['W']]
                rds = [('pbuf', sl), 'pszero', ('Cbuf', psl)]
            P.op('dve', lambda e: e.tensor_tensor_scan(out=cb[0:R, 512 - W:513][:, ::-1], data0=pb[0:R, 512 - W:513][:, ::-1],
                                                       data1=psf(6)[0:R, 0:1].to_broadcast([R, W + 1]), initial=init,
                                                       op0=ALU.mult, op1=ALU.add),
                 reads=rds, writes=[('Cbuf', s4)])
            P.op('pool', lambda e: e.tensor_tensor(out=wb[0:R, 0:W], in0=cb[0:R, 513 - W:513], in1=cb[0:R, 512 - W:512],
                                                   op=ALU.subtract),
                 reads=[('Cbuf', s4)], writes=[('wbuf', s5)])

        def s45(self, i):
            it = self.items[i]
            R, W = it['R'], it['W']
            sl = i % 4
            s5 = i % 5
            wb = self.B['wbuf'][s5]; wT = self.B['wT'][sl]
            tb = 3 + (i % 2)
            tp = r3(psb(tb), 128)
            nkb = (W + 127) // 128
            for kb in range(nkb):
                kw = min(128, W - kb * 128)
                w_in_ = wb[0:R, kb:W:4] if it.get('il') else wb[0:R, kb * 128:kb * 128 + kw]
                P.op('pe', lambda e, kb=kb, kw=kw, w_in_=w_in_: e.transpose(out=tp[0:kw, kb, 0:R], in_=w_in_,
                                                                            identity=identB[0:R, 0:R]),
                     reads=[('wbuf', s5), 'identB'], writes=[('ps', tb)])
            kw0 = min(128, W)
            if True:
                P.op('act', lambda e: e.activation(out=wT[0:kw0, 0:nkb, 0:R], in_=tp[0:kw0, 0:nkb, 0:R], func=AF.Copy),
                     reads=[('ps', tb)], writes=[('wT', sl)])
            else:
                P.op('dve', lambda e: e.tensor_copy(out=wT[0:kw0, 0:nkb, 0:R], in_=tp[0:kw0, 0:nkb, 0:R]),
                     reads=[('ps', tb)], writes=[('wT', sl)])

        def s6(self, i):
            it = self.items[i]
            R, W = it['R'], it['W']
            sl = i % 4
            wT = self.B['wT'][sl]
            ob = it['obank']; h = it['h']
            nkb = (W + 127) // 128
            for kb in range(nkb):
                kw = min(128, W - kb * 128)
                vap = it['v'](kb)
                P.op('pe', lambda e, kb=kb, kw=kw, vap=vap: e.matmul(psf(ob)[0:R, h * HD:(h + 1) * HD], lhsT=wT[0:kw, kb, 0:R], rhs=vap,
                                                                     start=(it['first'] and kb == 0),
                                                                     stop=(it['last'] and kb == nkb - 1)),
                     reads=[('wT', sl)] + it['vrd'], writes=[('ps', ob)])
            if it.get('post'):
                it['post'](i + 6)

    def epilogue(pipe, B, R, ob, cT_fn, crd, xres_ap, xrd, dst_ap, uid):
        osb, sq, onb, onT = B['osb'], B['sq'], B['onb'], B['onT']
        x2s = B['x2s'][uid % 2]

        def e1():
            P.op('act', lambda e: e.activation(out=osb[0:R, :], in_=psf(ob)[0:R, :], func=AF.Copy),
                 reads=[('ps', ob)], writes=['osb'])
            P.op('dve', lambda e: e.tensor_tensor(out=sq[0:R, :], in0=osb[0:R, :], in1=osb[0:R, :], op=ALU.mult),
                 reads=['osb'], writes=['sq', 'onb'])
            P.op('dve', lambda e: e.tensor_reduce(out=sm[0:R, 8:16], in_=r3(sq, HD)[0:R], axis=AX.X, op=ALU.add),
                 reads=['sq', 'onb'], writes=['ss8'])
            rstd_pool(sm[0:R, 8:16], sm[0:R, 8:16], 1.0 / HD, 8, ['ss8'], ['ss8'])
            P.op('dve', lambda e: e.tensor_tensor(out=r3(onb, HD)[0:R], in0=r3(osb, HD)[0:R],
                                                  in1=sm[0:R, 8:16].unsqueeze(2).to_broadcast([R, NH, HD]), op=ALU.mult),
                 reads=['osb', 'ss8'], writes=['onb'])
            dump('osb', osb, [128, 512], ['osb'])
            dump('onb', onb, [128, 512], ['onb'], BF16)

        def e2():
            tp = r3(psb(7), 128)
            for j in range(4):
                P.op('pe', lambda e, j=j: e.transpose(out=tp[:, j, 0:R], in_=onb[0:R, j * 128:(j + 1) * 128],
                                                      identity=identB[0:R, 0:R]),
                     reads=['onb', 'identB'], writes=[('ps', 7)])
            P.op('act', lambda e: e.activation(out=onT[:, :, 0:R], in_=tp[:, 0:4, 0:R], func=AF.Copy),
                 reads=[('ps', 7)], writes=['onT'])
            for nh in range(2):
                bank = 3 + nh
                bank = 0 if nh == 0 else 7
                for j in range(8):
                    lhs = onT[:, j, 0:R] if j < 4 else cT_fn(j - 4)
                    P.op('pe', lambda e, j=j, lhs=lhs, bank=bank, nh=nh: e.matmul(
                        psf(bank)[0:R, :], lhsT=lhs, rhs=wout[:, j, nh * 512:(nh + 1) * 512], start=(j == 0), stop=(j == 7)),
                        reads=['onT', ('wout', j)] + crd, writes=[('ps', bank)])
                P.op('dve', lambda e, bank=bank, nh=nh: e.tensor_tensor(out=x2s[0:R, nh * 512:(nh + 1) * 512], in0=psf(bank)[0:R, :],
                                                                        in1=xres_ap[:, nh * 512:(nh + 1) * 512], op=ALU.add),
                     reads=[('ps', bank)] + xrd, writes=[('x2s', 0)])
            dump('onT', onT.rearrange("p a b -> p (a b)"), [128, 512], ['onT'], BF16)
            dump('x2s', x2s, [128, 1024], [('x2s', 0)])
            P.dma('sp', lambda e: e.dma_start(out=dst_ap, in_=x2s[0:R, :]), reads=[('x2s', 0)], key=('x2s', 0))
        return e1, e2

    def p2_prompt(B, b):
        pipe = AttnPipe(B)

        def loads(i):
            s = i % 2
            P.dma('sp', lambda e: e.dma_start(out=B['qe'][s][0:64, :, :],
                                              in_=qTd[b].rearrange("g p t -> p g t")[0:64, :, i * 128:(i + 1) * 128]),
                  writes=[('qblk', s)], key=('qblk', s))
            P.dma('sp', lambda e: e.dma_start(out=B['qo'][s][64:128, :, :],
                                              in_=qTd[b].rearrange("g p t -> p g t")[64:128, :, i * 128:(i + 1) * 128]),
                  writes=[('qblk', s)], key=('qblk', s))

        def loads_xc(i):
            s = i % 2
            P.dma('sp', lambda e: e.dma_start(out=B['cblk'][s], in_=cTd[b].rearrange("g p t -> p g t")[:, :, i * 128:(i + 1) * 128]),
                  writes=[('cblk', s)], key=('cblk', s))
            r0 = b * T + i * 128
            P.dma('sp', lambda e: e.dma_start(out=B['xres'][s], in_=xp[r0:r0 + 128, :]), writes=[('xres', s)], key=('xres', s))

        loads_xc(0)
        loads(0)
        for i in range(T // 128):
            s = i % 2
            ob = 5
            nk = (i + 1) * 128
            c_hi = (nk - 1) // 512
            for h in range(NH):
                hp, po = h // 2, (h % 2) * 64
                for ci, c in enumerate(range(c_hi, -1, -1)):
                    c0 = c * 512
                    W = min(512, nk - c0)
                    it = dict(R=128, W=W, h=h, obank=ob, first=(ci == 0), last=(c == 0), masked=(ci == 0), mw=128,
                              qT=(B['qe'] if h % 2 == 0 else B['qo'])[s][:, hp, :], kT=kT[:, hp, c0:c0 + W],
                              rd=[('qblk', s)] + [('kT', (c0 + x) // GT) for x in range(0, W, GT)],
                              v=(lambda kb, c0=c0, h=h: vv[:, c0 // 128 + kb, h * HD:(h + 1) * HD]),
                              vrd=[('vv', c0 // 128 + kb) for kb in range((W + 127) // 128)])
                    if h == 0 and ci == 0 and i + 1 < T // 128:
                        it['pre'] = (lambda i=i: loads(i + 1))
                    if h == NH - 1 and c == 0:
                        def post(step, i=i, s=s, ob=ob):
                            r0 = b * T + i * 128
                            e1, e2 = epilogue(pipe, B, 128, ob, lambda j: B['cblk'][s][:, j, :], [('cblk', s)],
                                              B['xres'][s], [('xres', s)], x2d[r0:r0 + 128, :], i)
                            e1()
                            if i + 1 < T // 128:
                                loads_xc(i + 1)
                            pipe.defer(step + 2, e2)
                        it['post'] = post
                    pipe.add(it)
        pipe.run()

    def p3():
        SB3 = 256
        wg = r3(A.bf(8 * DFF), DFF); wu = r3(A.bf(8 * DFF), DFF); wd = r3(A.bf(NF * D), D)
        fgb = A.f32(D)
        mark3 = A.top
        wst3 = [A.f32(DFF) for _ in range(4)]
        A.top = mark3
        for kc in range(8):
            load_cast(wg[:, kc, :], w_gate[kc * 128:(kc + 1) * 128, :], DFF, g2[:, kc:kc + 1], wst3, 'wst3')
            load_cast(wu[:, kc, :], w_up[kc * 128:(kc + 1) * 128, :], DFF, g2[:, kc:kc + 1], wst3, 'wst3')
        for f in range(NF):
            load_cast(wd[:, f, :], w_down[f * 128:(f + 1) * 128, :], D, None, wst3, 'wst3')
        P.dma('sp', lambda e: e.dma_start(out=fgb, in_=fg_d.partition_broadcast(128)), writes=['fgb'], key='k_fgb')
        P.barrier()
        x2b = [[A.f32(D) for _ in range(SB3 // 128)] for _ in range(2)]
        xn = [A.bf(D) for _ in range(2)]
        hfT = [r3(A.bf(8 * SB3), SB3) for _ in range(2)]
        act = r3(A.bf(NF * SB3), SB3)
        sg = [A.f32(SB3) for _ in range(2)]
        yf = [A.f32(D) for _ in range(2)]
        WG = []; WU = []

        blocks = []
        for i in range(2 * T // 128):
            blocks.append((i * 128, 128, yp[i * 128:(i + 1) * 128, :]))
        nb3 = SB3 // 128
        sbs = [blocks[i:i + nb3] for i in range(0, len(blocks), nb3)]
        sbs.append([(2 * T, 2 * ST, ys[:, :])])

        def cols_of(sbl):
            c = 0; out = []
            for (_, rows, _) in sbl:
                out.append(c); c += rows
            return out

        def norm_pre3(sbi):
            st_ = sbi % 2
            for j, (r0, rows, dst) in enumerate(sbs[sbi]):
                xb = x2b[st_][j]
                P.dma('pool', lambda e, xb=xb, r0=r0, rows=rows: e.dma_start(out=xb[0:rows, :], in_=x2d[r0:r0 + rows, :]),
                      writes=[('x2b', st_, j)], key=('x2b', st_, j))
                P.op('act', lambda e, xb=xb, rows=rows, j=j: e.activation(out=xn[j][0:rows, :], in_=xb[0:rows, :], func=AF.Square,
                                                                          accum_out=sm[0:rows, j:j + 1]),
                     reads=[('x2b', st_, j)], writes=[('xn', j), ('ss', j)])
                P.op('act', lambda e, rows=rows, j=j: e.activation(out=sm[0:rows, 2 + j:3 + j], in_=sm[0:rows, j:j + 1],
                                                                   func=AF.Sqrt, scale=1.0 / D, bias=epsT[0:rows, :]),
                     reads=[('ss', j), 'epsT'], writes=[('rs', j)])
                P.op('dve', lambda e, rows=rows, j=j: e.reciprocal(out=sm[0:rows, 2 + j:3 + j], in_=sm[0:rows, 2 + j:3 + j]),
                     reads=[('rs', j)], writes=[('rs', j)])
                P.op('dve', lambda e, xb=xb, rows=rows, j=j: e.tensor_scalar(out=xn[j][0:rows, :], in0=xb[0:rows, :],
                                                                             scalar1=sm[0:rows, 2 + j:3 + j], scalar2=None,
                                                                             op0=ALU.mult),
                     reads=[('x2b', st_, j), ('rs', j)], writes=[('xn', j)])

        def norm_post3(sbi):
            st_ = sbi % 2
            cols = cols_of(sbs[sbi])
            for j, (r0, rows, dst) in enumerate(sbs[sbi]):
                tp = r3(psb(0), 128)
                for kc in range(8):
                    P.op('pe', lambda e, kc=kc, rows=rows, j=j, tp=tp: e.transpose(out=tp[:, kc, 0:rows],
                                                                                  in_=xn[j][0:rows, kc * 128:(kc + 1) * 128],
                                                                                  identity=identB[0:rows, 0:rows]),
                         reads=[('xn', j), 'identB'], writes=[('ps', 0)])
                col = cols[j]
                P.op('act', lambda e, col=col, rows=rows, tp=tp: e.activation(out=hfT[st_][:, :, col:col + rows], in_=tp[:, :, 0:rows],
                                                                              func=AF.Copy),
                     reads=[('ps', 0)], writes=[('hfT', st_)])

        norm_pre3(0)
        norm_post3(0)
        for sbi, sbl in enumerate(sbs):
            st_ = sbi % 2
            n = sum(r for _, r, _ in sbl)
            cols = cols_of(sbl)
            hf = hfT[st_]
            if sbi + 1 < len(sbs):
                norm_pre3(sbi + 1)
            for f in range(NF):
                gb = 1 + 2 * (f % 2); ub_ = gb + 1
                for kc in range(8):
                    P.op('pe', lambda e, kc=kc, f=f, gb=gb, n=n, hf=hf: e.matmul(psf(gb)[:, 0:n], lhsT=wg[:, kc, f * 128:(f + 1) * 128],
                                                                                 rhs=hf[:, kc, 0:n], start=(kc == 0), stop=(kc == 7)),
                         reads=[('hfT', st_)] + WG, writes=[('ps', gb)])
                for kc in range(8):
                    P.op('pe', lambda e, kc=kc, f=f, ub_=ub_, n=n, hf=hf: e.matmul(psf(ub_)[:, 0:n], lhsT=wu[:, kc, f * 128:(f + 1) * 128],
                                                                                   rhs=hf[:, kc, 0:n], start=(kc == 0), stop=(kc == 7)),
                         reads=[('hfT', st_)] + WU, writes=[('ps', ub_)])
                P.op('act', lambda e, f=f, gb=gb, n=n: e.activation(out=sg[f % 2][:, 0:n], in_=psf(gb)[:, 0:n], func=AF.Silu),
                     reads=[('ps', gb)], writes=[('sg', f % 2)])
                P.op('dve', lambda e, f=f, ub_=ub_, n=n: e.tensor_tensor(out=act[:, f, 0:n], in0=psf(ub_)[:, 0:n], in1=sg[f % 2][:, 0:n],
                                                                         op=ALU.mult),
                     reads=[('ps', ub_), ('sg', f % 2)], writes=['act'])
            if sbi + 1 < len(sbs):
                norm_post3(sbi + 1)
            for j, (r0, rows, dst) in enumerate(sbl):
                xb = x2b[st_][j]; yo = yf[j % 2]; c0 = cols[j]
                for nh in range(2):
                    bank = 5 + nh
                    for f in range(NF):
                        P.op('pe', lambda e, f=f, nh=nh, bank=bank, rows=rows, c0=c0: e.matmul(
                            psf(bank)[0:rows, :], lhsT=act[:, f, c0:c0 + rows], rhs=wd[:, f, nh * 512:(nh + 1) * 512],
                            start=(f == 0), stop=(f == NF - 1)), reads=['act'], writes=[('ps', bank)])
                    P.op('dve', lambda e, nh=nh, bank=bank, rows=rows, xb=xb, yo=yo: e.tensor_tensor(
                        out=yo[0:rows, nh * 512:(nh + 1) * 512], in0=psf(bank)[0:rows, :], in1=xb[0:rows, nh * 512:(nh + 1) * 512],
                        op=ALU.add), reads=[('ps', bank), ('x2b', st_, j)], writes=[('yf', j % 2)])
                sl = 4 + (j % 2)
                P.op('act', lambda e, rows=rows, yo=yo, sl=sl, xb=xb: e.activation(out=xb[0:rows, :], in_=yo[0:rows, :],
                                                                                 func=AF.Square, accum_out=sm[0:rows, sl:sl + 1]),
                     reads=[('yf', j % 2)], writes=[('x2b', st_, j), ('ss', sl)])
                P.op('act', lambda e, rows=rows, sl=sl: e.activation(out=sm[0:rows, sl + 2:sl + 3], in_=sm[0:rows, sl:sl + 1],
                                                                     func=AF.Sqrt, scale=1.0 / D, bias=epsT[0:rows, :]),
                     reads=[('ss', sl), 'epsT'], writes=[('rs', sl)])
                P.op('dve', lambda e, rows=rows, sl=sl: e.reciprocal(out=sm[0:rows, sl + 2:sl + 3], in_=sm[0:rows, sl + 2:sl + 3]),
                     reads=[('rs', sl)], writes=[('rs', sl)])
                P.op('dve', lambda e, rows=rows, yo=yo, sl=sl: e.scalar_tensor_tensor(
                    out=yo[0:rows, :], in0=yo[0:rows, :], scalar=sm[0:rows, sl + 2:sl + 3], in1=fgb[0:rows, :],
                    op0=ALU.mult, op1=ALU.mult), reads=[('yf', j % 2), ('rs', sl), 'fgb'], writes=[('yf', j % 2)])
                P.dma('sp', lambda e, rows=rows, yo=yo, dst=dst: e.dma_start(out=dst, in_=yo[0:rows, :]),
                      reads=[('yf', j % 2)], key=('yf', j % 2))

    def p_sample(B1, S):
        n = 2 * ST
        B = B1
        norm_block(B, 0, xs[:, :], n, 0, 0, 'x')
        hT = B['hT'][0]
        for fg in range(4):
            bank = gbank()
            mm_group(bank, n, lambda kc: win[:, kc, fg * 128:(fg + 1) * 128], lambda kc: hT[:, kc, 0:n], 8, [('hT', 0)] + WIN_ALL)
            evac(S['qTe'][0:64, fg, :], psf(bank)[0:64, 0:n], [('ps', bank)], ['qTn'])
            evac(S['qTo'][64:128, fg, :], psf(bank)[64:128, 0:n], [('ps', bank)], ['qTn'])
        for fg in range(4):
            bank = gbank()
            mm_group(bank, n, lambda kc: win[:, kc, 512 + fg * 128:512 + (fg + 1) * 128], lambda kc: hT[:, kc, 0:n], 8,
                     [('hT', 0)] + WIN_ALL)
            evac(S['kTn'][:, fg, :], psf(bank)[:, 0:n], [('ps', bank)], ['kTn'])
        bank = gbank()
        mm_group(bank, 512, lambda kc: hT[:, kc, 0:n], lambda kc: win[:, kc, 512:1024], 8, [('hT', 0)] + WIN_ALL, m=n)
        evac(B['kst'][0][0:n, :], psf(bank)[0:n, :], [('ps', bank)], [('kst', 0)])
        for b in range(2):
            P.dma('sp', lambda e, b=b: e.dma_start(out=ks[b].rearrange("h t d -> t h d"),
                                                   in_=r3(B['kst'][0], HD)[b * ST:(b + 1) * ST]),
                  reads=[('kst', 0)], key=('kst', 0))
        for b in range(2):
            bank = gbank()
            mm_group(bank, 512, lambda kc: hT[:, kc, b * ST:(b + 1) * ST], lambda kc: win[:, kc, 1024:1536], 8,
                     [('hT', 0)] + WIN_ALL, m=ST)
            evac(B['vst'][b][0:ST, :], psf(bank)[0:ST, :], [('ps', bank)], [('vst', b)])
            P.op('pool', lambda e, b=b: e.tensor_copy(out=S['vn'][b][0:ST, :], in_=B['vst'][b][0:ST, :]),
                 reads=[('vst', b)], writes=[('vn', b)])
            P.dma('sp', lambda e, b=b: e.dma_start(out=vs[b].rearrange("h t d -> t h d"), in_=r3(B['vst'][b], HD)[0:ST]),
                  reads=[('vst', b)], key=('vst', b))
        P.dma('sp', lambda e: e.dma_start(out=S['sct'][0:2 * CH, :], in_=sc[:, :]), writes=['sct'], key='sct')
        ub = B['ub'][0]
        seg_w = CH + ST
        for cg in range(4):
            P.op('pe', lambda e, cg=cg: e.transpose(out=psf(0)[:, 0:2 * CH], in_=S['sct'][0:2 * CH, cg * 128:(cg + 1) * 128],
                                                    identity=identF[0:2 * CH, 0:2 * CH]),
                 reads=['sct', 'identF'], writes=[('ps', 0)])
            for b in range(2):
                P.op('act', lambda e, cg=cg, b=b: e.activation(out=ub[:, cg, b * seg_w:b * seg_w + CH],
                                                               in_=psf(0)[:, b * CH:(b + 1) * CH], func=AF.Copy),
                     reads=[('ps', 0)], writes=[('ubh', 0)])
        p1_feature(B, 0, n, 0, [(0, ST, 0), (ST, ST, seg_w)], [B['uf32'], S['uf32b']])
        for cg in range(4):
            P.op('pool', lambda e, cg=cg: e.tensor_copy(out=S['cTn'][:, cg, :], in_=B['cTs'][:, cg, 0:n]),
                 reads=['cTs'], writes=['cTn'])
        conv_out(B, cs[0], 0, B['uf32'])
        conv_out(B, cs[1], ST, S['uf32b'])

    def sample_attn(S, B, b):
        GB = 8
        stg = [r3(A.t[:, win_off // 2 + i * GB * 1024: win_off // 2 + (i + 1) * GB * 1024].bitcast(F32), 512) for i in range(4)]
        for g in range(32 // GB):
            ks_ = stg[g % 2]; vs_ = stg[2 + g % 2]
            for h in range(NH):
                for cl in range(2):
                    P.dma('sp', lambda e, h=h, g=g, ks_=ks_, cl=cl: e.dma_start(
                        out=ks_[:, cl * 4:(cl + 1) * 4, h * HD:(h + 1) * HD],
                        in_=ck[b, h, g * GB * 128 + cl * 512:g * GB * 128 + (cl + 1) * 512, :].rearrange("(p j) d -> p j d", j=4)),
                        writes=[('kstg', g % 2)], key=('kstg', g % 2))
            for h in range(NH):
                for cl in range(2):
                    P.dma('sp', lambda e, h=h, g=g, vs_=vs_, cl=cl: e.dma_start(
                        out=vs_[:, cl * 4:(cl + 1) * 4, h * HD:(h + 1) * HD],
                        in_=cv[b, h, g * GB * 128 + cl * 512:g * GB * 128 + (cl + 1) * 512, :].rearrange("(p j) d -> p j d", j=4)),
                        writes=[('vstg', g % 2)], key=('vstg', g % 2))
            for j in range(GB):
                blk = g * GB + j
                tb = 3 + (blk % 2)
                for hp in range(4):
                    P.op('pe', lambda e, j=j, hp=hp, tb=tb, ks_=ks_: e.transpose(out=psf(tb)[:, hp * 128:(hp + 1) * 128],
                                                                             in_=ks_[:, j, hp * 128:(hp + 1) * 128], identity=identF),
                         reads=[('kstg', g % 2), 'identF'], writes=[('ps', tb)])
                c0_ = (g * GB + (j // 4) * 4) * 128
                evac(kT[:, :, c0_ + (j % 4):c0_ + 512:4], r3(psf(tb), 128), [('ps', tb)], ['kTc'])
                evac(vv[:, blk, :], vs_[:, j, :], [('vstg', g % 2)], ['vvc'])
        P.dma('sp', lambda e: e.dma_start(out=B['xres'][b][0:ST, :], in_=xs[b * ST:(b + 1) * ST, :]),
              writes=[('xres', b)], key=('xres', b))
        P.barrier()
        pipe = AttnPipe(B)
        ob = 5
        for h in range(NH):
            hp, po = h // 2, (h % 2) * 64
            qTa = (S['qTe'] if h % 2 == 0 else S['qTo'])[:, hp, b * ST:(b + 1) * ST]
            it = dict(R=ST, W=ST, h=h, obank=ob, first=True, last=False, masked=True, mw=ST,
                      qT=qTa, kT=S['kTn'][:, hp, b * ST:(b + 1) * ST], rd=[],
                      v=(lambda kb, h=h: S['vn'][b][0:ST, h * HD:(h + 1) * HD]), vrd=[])
            pipe.add(it)
            for c in range(7, -1, -1):
                c0 = c * 512
                it = dict(R=ST, W=512, h=h, obank=ob, first=False, last=(c == 0), masked=False, mw=0, il=True,
                          qT=qTa, kT=kT[:, hp, c0:c0 + 512], rd=[],
                          v=(lambda kb, c0=c0, h=h: vv[:, c0 // 128 + kb, h * HD:(h + 1) * HD]), vrd=[])
                if h == NH - 1 and c == 0:
                    def post(step):
                        r0 = 2 * T + b * ST
                        e1, e2 = epilogue(pipe, B, ST, ob, lambda j: S['cTn'][:, j, b * ST:(b + 1) * ST], [],
                                          B['xres'][b][0:ST, :], [('xres', b)], x2d[r0:r0 + ST, :], b)
                        e1()
                        pipe.defer(step + 2, e2)
                    it['post'] = post
                pipe.add(it)
        pipe.run()
        P.barrier()

    for b in range(2):
        A.top = phase_top
        B1 = p1_alloc()
        p1_prompt(B1, b)
        P.barrier()
        A.top = phase_top
        B2 = p2_alloc()
        p2_prompt(B2, b)
        P.barrier()
    A.top = phase_top
    S = {}
    S['kTn'] = r3(A.bf(4 * 64), 64)
    S['qTe'] = r3(A.bf(4 * 64), 64)
    S['qTo'] = r3(A.bf(4 * 64), 64)
    P.op('pool', lambda e: e.memset(S['qTe'][64:128, :, :], 0.0), writes=['qTn'])
    P.op('pool', lambda e: e.memset(S['qTo'][0:64, :, :], 0.0), writes=['qTn'])
    S['cTn'] = r3(A.bf(4 * 64), 64)
    S['vn'] = [A.bf(512) for _ in range(2)]
    S['sct'] = A.f32(512)
    S['uf32b'] = r3(A.f32(4 * 32), 32)
    s_top = A.top
    B1 = p1_alloc(1)
    p_sample(B1, S)
    P.barrier()
    A.top = s_top
    B2 = p2_alloc()
    for b in range(2):
        sample_attn(S, B2, b)
    A.top = base_top
    p3()
    P.barrier()

    with nc.Block() as block:
        P.emit(block)
    stack.close()
    return nc


_CACHE = {}


def _consts():
    c = np.zeros((128, 384), np.float32)
    c[:, 0:128] = np.eye(128, dtype=np.float32)
    i = np.arange(128)
    c[:, 128:256] = np.where(i[None, :] >= i[:, None], NEG, 0.0).astype(np.float32)
    c[:, 256:384] = 1.0
    return c


def kernel(x_prompt, x_sample, cache_k, cache_v, state_conv, w_in, sb_norm_g, conv_w, conv_b, conv_ln_g,
           conv_ln_b, w_out, norm1_g, norm2_g, w_gate, w_up, w_down, final_g, _ncores=8):
    f = lambda a: np.ascontiguousarray(np.asarray(a, dtype=np.float32))
    nc = bass.Bass("TRN2", target_bir_lowering=False)
    build_program(nc)
    cst = _consts()
    shared = dict(w_in=f(w_in[0]), sbg=f(sb_norm_g[0]).reshape(512), convw=f(conv_w[0]), convb=f(conv_b[0]),
                  lng=f(conv_ln_g[0]), lnb=f(conv_ln_b[0]), w_out=f(w_out[0]), g1=f(norm1_g[0]), g2=f(norm2_g[0]),
                  w_gate=f(w_gate[0]), w_up=f(w_up[0]), w_down=f(w_down[0]), fg=f(final_g), cst=cst)
    in_maps = []
    for c in range(_ncores):
        m = dict(shared)
        m['xp'] = f(x_prompt[2 * c:2 * c + 2]).reshape(2 * T, D)
        m['xs'] = f(x_sample[2 * c:2 * c + 2]).reshape(2 * ST, D)
        m['ck'] = f(cache_k[0, 2 * c:2 * c + 2])
        m['cv'] = f(cache_v[0, 2 * c:2 * c + 2])
        m['sc'] = f(state_conv[0, 2 * c:2 * c + 2]).reshape(2 * CH, 512)
        in_maps.append(m)
    res = run_bass_kernel_spmd(nc, in_maps, core_ids=list(range(_ncores)))
    R = res.results
    if DEBUG:
        DBG['r0'] = R[0]
    cat = lambda k, shp: np.concatenate([np.asarray(r[k]).reshape(shp) for r in R], axis=0)
    y_prompt = cat('yp', (2, T, D))
    y_sample = cat('ys', (2, ST, D))
    k_prompt = cat('kp', (2, NH, T, HD))[None]
    v_prompt = cat('vp', (2, NH, T, HD))[None]
    conv_prompt = cat('cp', (2, CH, 512))[None]
    k_sample = cat('ks', (2, NH, ST, HD))[None]
    v_sample = cat('vs', (2, NH, ST, HD))[None]
    conv_sample = cat('cs', (2, CH, 512))[None]
    return (y_prompt, y_sample, k_prompt, v_prompt, conv_prompt, k_sample, v_sample, conv_sample)
```

```python
import numpy as np
from contextlib import ExitStack
import concourse.bass as bass
import concourse.mybir as mybir
from concourse.bass_utils import run_bass_kernel_spmd

F32 = mybir.dt.float32
BF16 = mybir.dt.bfloat16
AF = mybir.ActivationFunctionType
ALU = mybir.AluOpType
AX = mybir.AxisListType

T = 4096
D = 1024
NH = 8
HD = 64
DIN = 2560
DFF = 2816
NF = DFF // 128
CW = 31
CH = 30
EPS = 1e-6
GT = 256
NB1 = GT // 128
ST = 32
NEG = -30000.0
SAME_SYNC = True
PARANOID = False
DEBUG = False
DBG = {}
ARENA_BYTES = 212736


class Prog:
    def __init__(self, nc, stack):
        self.nc, self.stack = nc, stack
        self.engs = ['pe', 'act', 'dve', 'pool', 'sp']
        self.q = {e: [] for e in self.engs}
        self.sems = []
        self.cur = {}
        self.cnt = {}
        for e in self.engs:
            self._newsem(e)
        self.lastw = {}
        self.rd = {}
        self.waited = {e: {} for e in self.engs}
        self.dsem = {}

    def _alloc(self, name):
        h = self.stack.enter_context(self.nc.semaphore(name))
        self.sems.append(h)
        return len(self.sems) - 1

    def _newsem(self, e):
        self.cur[e] = self._alloc(f"s{e}{len(self.sems)}")
        self.cnt[e] = 0

    def _need(self, eng, tok, waits, war=False, force=False):
        s, v, te = tok
        if te == eng and not force:
            if eng == 'pe' or war or not SAME_SYNC:
                return
        if self.waited[eng].get(s, 0) >= v:
            return
        self.waited[eng][s] = v
        for i, (s2, v2) in enumerate(waits):
            if s2 == s:
                waits[i] = (s, max(v, v2))
                return
        waits.append((s, v))

    def _deps(self, eng, reads, writes):
        waits = []
        for r in reads:
            t = self.lastw.get(r)
            if t:
                self._need(eng, t, waits)
        for w in writes:
            t = self.lastw.get(w)
            if t:
                self._need(eng, t, waits)
            for t in self.rd.get(w, {}).values():
                self._need(eng, t, waits, war=True)
        return waits

    def _commit(self, tok, reads, writes):
        for r in reads:
            self.rd.setdefault(r, {})[tok[0]] = tok
        for w in writes:
            self.lastw[w] = tok
            self.rd[w] = {}

    def _paranoid(self, eng, waits):
        for e in self.engs:
            if self.cnt[e] > 0:
                self._need(eng, (self.cur[e], self.cnt[e], e), waits, force=(e != 'pe' or eng != 'pe'))
        for d in self.dsem.values():
            self._need(eng, (d[0], d[1], None), waits)

    def op(self, eng, fn, reads=(), writes=()):
        waits = self._deps(eng, reads, writes)
        if PARANOID:
            self._paranoid(eng, waits)
        if self.cnt[eng] >= 60000:
            self._newsem(eng)
        self.cnt[eng] += 1
        tok = (self.cur[eng], self.cnt[eng], eng)
        self._commit(tok, reads, writes)
        self.q[eng].append((waits, fn, self.cur[eng], 1))

    def dma(self, eng, fn, reads=(), writes=(), key=None):
        waits = self._deps(eng, reads, writes)
        if PARANOID:
            self._paranoid(eng, waits)
        if key not in self.dsem or self.dsem[key][1] >= 60000:
            self.dsem[key] = [self._alloc(f"d{len(self.sems)}"), 0]
        d = self.dsem[key]
        d[1] += 16
        tok = (d[0], d[1], None)
        self._commit(tok, reads, writes)
        self.q[eng].append((waits, fn, d[0], 16))

    def barrier(self):
        toks = [(self.cur[e], self.cnt[e], e) for e in self.engs if self.cnt[e] > 0]
        toks += [(d[0], d[1], None) for d in self.dsem.values()]
        for e in self.engs:
            waits = []
            for t in toks:
                self._need(e, t, waits, force=True)
            self.q[e].append((waits, None, None, 0))
        self.lastw.clear()
        self.rd.clear()

    def emit(self, block):
        engmap = {'pe': block.tensor, 'act': block.scalar, 'dve': block.vector,
                  'pool': block.gpsimd, 'sp': block.sync}
        for e in self.engs:
            items = self.q[e]

            def body(eng, items=items):
                for waits, fn, sem, inc in items:
                    for s, v in waits:
                        eng.wait_ge(self.sems[s], v)
                    if fn is not None:
                        fn(eng).then_inc(self.sems[sem], inc)
            engmap[e](body)


class Arena:
    def __init__(self, nc, stack, nbytes):
        self.t = stack.enter_context(nc.sbuf_tensor("arena", [128, nbytes // 2], BF16))
        self.size = nbytes
        self.top = 0

    def alloc(self, nbytes):
        off = self.top
        self.top += (nbytes + 63) // 64 * 64
        assert self.top <= self.size, f"SBUF arena overflow {self.top} > {self.size}"
        return off

    def bf(self, n):
        off = self.alloc(n * 2)
        return self.t[:, off // 2: off // 2 + n]

    def f32(self, n):
        off = self.alloc(n * 4)
        return self.t[:, off // 2: off // 2 + 2 * n].bitcast(F32)


def r3(ap, b):
    return ap.rearrange("p (a b) -> p a b", b=b)


def build_program(nc):
    stack = ExitStack()
    P = Prog(nc, stack)
    A = Arena(nc, stack, ARENA_BYTES)

    def din(name, shape):
        return nc.dram_tensor(name, list(shape), F32, kind="ExternalInput").ap()

    def dout(name, shape):
        return nc.dram_tensor(name, list(shape), F32, kind="ExternalOutput").ap()

    xp = din("xp", [2 * T, D]); xs = din("xs", [2 * ST, D])
    ck = din("ck", [2, NH, T, HD]); cv = din("cv", [2, NH, T, HD])
    sc = din("sc", [2 * CH, 512])
    w_in = din("w_in", [D, DIN]); sbg_d = din("sbg", [512]); cw_d = din("convw", [CW, 512])
    cb_d = din("convb", [512]); lg_d = din("lng", [512]); lb_d = din("lnb", [512])
    w_out = din("w_out", [D, D]); g1_d = din("g1", [D]); g2_d = din("g2", [D])
    w_gate = din("w_gate", [D, DFF]); w_up = din("w_up", [D, DFF]); w_down = din("w_down", [DFF, D])
    fg_d = din("fg", [D]); cst_d = din("cst", [128, 384])
    yp = dout("yp", [2 * T, D]); ys = dout("ys", [2 * ST, D])
    kp = dout("kp", [2, NH, T, HD]); vp = dout("vp", [2, NH, T, HD]); cp = dout("cp", [2, CH, 512])
    ks = dout("ks", [2, NH, ST, HD]); vs = dout("vs", [2, NH, ST, HD]); cs = dout("cs", [2, CH, 512])
    dk = dict(kind="ExternalOutput") if DEBUG else {}
    x2d = nc.dram_tensor("x2d", [2 * T + 2 * ST, D], F32, **dk).ap()
    qTd = nc.dram_tensor("qTd", [2, 4, 128, T], BF16, **dk).ap()
    cTd = nc.dram_tensor("cTd", [2, 4, 128, T], BF16, **dk).ap()

    ps = [stack.enter_context(nc.psum_tensor(f"ps{i}", [128, 512], F32)) for i in range(8)]

    def psf(i):
        return ps[i][:, :]

    def psb(i):
        return ps[i][:, :].bitcast(BF16)

    dbg_n = [0]

    def dump(name, ap, shape, reads, dt=F32):
        if not DEBUG or name in DBG.get('_done', set()):
            return
        DBG.setdefault('_done', set()).add(name)
        t = nc.dram_tensor("dbg_" + name, list(shape), dt, kind="ExternalOutput").ap()
        P.dma('sp', lambda e: e.dma_start(out=t, in_=ap), reads=reads, key=('dbg', name))

    identF = A.f32(128); identB = A.bf(128); negB = A.bf(128); onesB = A.bf(128)
    epsT = A.f32(1)
    pv = A.f32(160)
    pvA = A.f32(128); pvC = A.f32(128)
    g1 = pv[:, 0:8]; g2 = pv[:, 8:16]; sbg = pv[:, 16:20]; cbv = pv[:, 20:24]
    lgv = pv[:, 24:28]; lbv = pv[:, 28:32]
    sm = A.f32(64)
    mh = A.f32(8)
    base_top = A.top

    P.dma('sp', lambda e: e.dma_start(out=identF, in_=cst_d[:, 0:128]), writes=['identF'], key='k_identF')
    P.dma('pool', lambda e: e.dma_start(out=identB, in_=cst_d[:, 0:128]), writes=['identB'], key='k_identB')
    P.dma('pool', lambda e: e.dma_start(out=negB, in_=cst_d[:, 128:256]), writes=['negB'], key='k_negB')
    P.dma('pool', lambda e: e.dma_start(out=onesB, in_=cst_d[:, 256:384]), writes=['onesB'], key='k_onesB')
    P.op('pool', lambda e: e.memset(epsT, EPS), writes=['epsT'])
    P.op('pool', lambda e: e.memset(mh, -0.5), writes=['mh'])

    def rstd_pool(dst, src, scale, w, rd, wr):
        P.op('pool', lambda e: e.tensor_scalar(out=dst, in0=src, scalar1=scale, scalar2=EPS, op0=ALU.mult, op1=ALU.add),
             reads=rd, writes=wr)
        P.op('pool', lambda e: e.tensor_tensor(out=dst, in0=dst, in1=mh[0:dst.shape[0], 0:w], op=ALU.pow),
             reads=wr + ['mh'], writes=wr)
    for i, v in enumerate([g1_d, g2_d]):
        P.dma('sp', lambda e, i=i, v=v: e.dma_start(out=pvA[8 * i:8 * i + 8, :], in_=v.rearrange("(k p) -> k p", p=128)),
              writes=['pvA'], key='k_pvA')
    for i, v in enumerate([sbg_d, cb_d, lg_d, lb_d]):
        P.dma('sp', lambda e, i=i, v=v: e.dma_start(out=pvA[16 + 4 * i:20 + 4 * i, :], in_=v.rearrange("(k p) -> k p", p=128)),
              writes=['pvA'], key='k_pvA')
    P.dma('sp', lambda e: e.dma_start(out=pvC[0:124, :], in_=cw_d.rearrange("k (g p) -> (k g) p", p=128)),
          writes=['pvC'], key='k_pvC')
    P.op('pe', lambda e: e.transpose(out=psf(0)[:, 0:32], in_=pvA[0:32, :], identity=identF[0:32, 0:32]),
         reads=['pvA', 'identF'], writes=[('ps', 0)])
    P.op('pe', lambda e: e.transpose(out=psf(0)[:, 32:156], in_=pvC[0:124, :], identity=identF[0:124, 0:124]),
         reads=['pvC', 'identF'], writes=[('ps', 0)])
    P.op('dve', lambda e: e.tensor_copy(out=pv[:, 0:156], in_=psf(0)[:, 0:156]), reads=[('ps', 0)], writes=['pv'])

    kT = r3(A.bf(4 * T), T)
    vv = r3(A.bf(32 * 512), 512)
    win_off = A.top
    win = r3(A.bf(8 * DIN), DIN)
    convD = A.bf(124 * 128)
    wout = r3(A.bf(8 * D), D)
    phase_top = A.top

    wst = [A.f32(DIN) for _ in range(4)]
    A.top = phase_top
    widx = [0]

    def load_cast(dst, src, n, scale_ap, stgs, tag):
        i = widx[0]; widx[0] += 1
        st_ = stgs[i % len(stgs)]
        sk = (tag, i % len(stgs))
        P.dma('sp', lambda e: e.dma_start(out=st_[:, 0:n], in_=src), writes=[sk], key=sk)
        if i % 2 == 0:
            if scale_ap is None:
                P.op('act', lambda e: e.activation(out=dst, in_=st_[:, 0:n], func=AF.Identity), reads=[sk], writes=[])
            else:
                P.op('act', lambda e: e.activation(out=dst, in_=st_[:, 0:n], func=AF.Identity, scale=scale_ap),
                     reads=[sk, 'pv'], writes=[])
        else:
            if scale_ap is None:
                P.op('dve', lambda e: e.tensor_copy(out=dst, in_=st_[:, 0:n]), reads=[sk], writes=[])
            else:
                P.op('dve', lambda e: e.tensor_scalar(out=dst, in0=st_[:, 0:n], scalar1=scale_ap, scalar2=None, op0=ALU.mult),
                     reads=[sk, 'pv'], writes=[])

    for kc in range(8):
        load_cast(win[:, kc, :], w_in[kc * 128:(kc + 1) * 128, :], DIN, g1[:, kc:kc + 1], wst, 'wst')
    for kc in range(8):
        load_cast(wout[:, kc, :], w_out[kc * 128:(kc + 1) * 128, :], D, sbg[:, kc:kc + 1] if kc < 4 else None, wst, 'wst')
    for k in range(CW):
        for g in range(4):
            idx = k * 4 + g
            eng = 'dve' if idx % 2 == 0 else 'pool'
            P.op(eng, lambda e, idx=idx: e.tensor_scalar(out=convD[:, idx * 128:(idx + 1) * 128], in0=identF,
                                                         scalar1=pv[:, 32 + idx:33 + idx], scalar2=None, op0=ALU.mult),
                 reads=['pv', 'identF'], writes=['convD'])
    P.barrier()

    def p1_alloc(nx=2):
        B = {}
        B['xblk'] = [A.f32(D) for _ in range(nx)]
        B['xn'] = [A.bf(D) for _ in range(nx)]
        B['hT'] = [r3(A.bf(8 * GT), GT) for _ in range(2)]
        B['qTs'] = r3(A.bf(4 * GT), GT)
        B['cTs'] = r3(A.bf(4 * GT), GT)
        B['kst'] = [A.f32(512) for _ in range(2)]
        B['vst'] = [A.f32(512) for _ in range(2)]
        B['sig'] = [A.f32(GT) for _ in range(2)]
        B['ub'] = [r3(A.bf(4 * (CH + GT + 2)), CH + GT + 2) for _ in range(2)]
        B['cfp'] = r3(A.f32(4 * GT), GT)
        B['cb16'] = r3(A.bf(4 * GT), GT)
        B['csq'] = r3(A.bf(4 * GT), GT)
        B['mm'] = A.f32(GT); B['rs2'] = A.f32(GT); B['msq'] = A.f32(GT)
        B['uf32'] = r3(A.f32(4 * 32), 32)
        return B

    gen_rot = [0]

    def gbank():
        gen_rot[0] = (gen_rot[0] + 1) % 4
        return 1 + gen_rot[0]

    conv_rot = [0]

    def norm_pre(B, slot, src_ap, rows, xkey):
        xb = B['xblk'][slot]; xn = B['xn'][slot]
        P.dma('pool', lambda e: e.dma_start(out=xb[0:rows, :], in_=src_ap), writes=[('xblk', slot)], key=(xkey, slot))
        P.op('act', lambda e: e.activation(out=xn[0:rows, :], in_=xb[0:rows, :], func=AF.Square,
                                           accum_out=sm[0:rows, slot:slot + 1]),
             reads=[('xblk', slot)], writes=[('xn', slot), ('ss', slot)])
        P.op('act', lambda e: e.activation(out=sm[0:rows, 2 + slot:3 + slot], in_=sm[0:rows, slot:slot + 1], func=AF.Sqrt,
                                           scale=1.0 / D, bias=epsT[0:rows, :]),
             reads=[('ss', slot), 'epsT'], writes=[('rs', slot)])
        P.op('dve', lambda e: e.reciprocal(out=sm[0:rows, 2 + slot:3 + slot], in_=sm[0:rows, 2 + slot:3 + slot]),
             reads=[('rs', slot)], writes=[('rs', slot)])
        P.op('dve', lambda e: e.tensor_scalar(out=xn[0:rows, :], in0=xb[0:rows, :], scalar1=sm[0:rows, 2 + slot:3 + slot],
                                              scalar2=None, op0=ALU.mult),
             reads=[('xblk', slot), ('rs', slot)], writes=[('xn', slot)])

    def norm_post(B, slot, rows, hs, col0):
        xn = B['xn'][slot]; hT = B['hT'][hs]
        tp = r3(psb(0), 128)
        for kc in range(8):
            P.op('pe', lambda e, kc=kc: e.transpose(out=tp[:, kc, 0:rows], in_=xn[0:rows, kc * 128:(kc + 1) * 128],
                                                    identity=identB[0:rows, 0:rows]),
                 reads=[('xn', slot), 'identB'], writes=[('ps', 0)])
        P.op('act', lambda e: e.activation(out=hT[:, :, col0:col0 + rows], in_=tp[:, :, 0:rows], func=AF.Copy),
             reads=[('ps', 0)], writes=[('hT', hs)])

    def norm_block(B, slot, src_ap, rows, hs, col0, xkey):
        norm_pre(B, slot, src_ap, rows, xkey)
        norm_post(B, slot, rows, hs, col0)

    def mm_group(bank, n, lhs_fn, rhs_fn, nk, reads, m=128):
        for kc in range(nk):
            lhs = lhs_fn(kc); rhs = rhs_fn(kc)
            P.op('pe', lambda e, kc=kc, lhs=lhs, rhs=rhs: e.matmul(psf(bank)[0:m, 0:n], lhsT=lhs, rhs=rhs,
                                                                   start=(kc == 0), stop=(kc == nk - 1)),
                 reads=reads, writes=[('ps', bank)])

    WIN_ALL = [('win', kc) for kc in range(8)]
    evac_rot = [0]

    def evac(out_ap, in_ap, reads, writes):
        evac_rot[0] ^= 1
        if evac_rot[0]:
            P.op('act', lambda e: e.activation(out=out_ap, in_=in_ap, func=AF.Copy), reads=reads, writes=writes)
        else:
            P.op('dve', lambda e: e.tensor_copy(out=out_ap, in_=in_ap), reads=reads, writes=writes)

    def p1_feature(B, hs, n, us, segs, u32bufs, split=False):
        hT = B['hT'][hs]; ub = B['ub'][us]
        for cg in range(4):
            ab, gb_ = (3, 4) if cg % 2 == 0 else (1, 2)
            mm_group(ab, n, lambda kc: win[:, kc, 1536 + cg * 128:1536 + (cg + 1) * 128], lambda kc: hT[:, kc, 0:n], 8,
                     [('hT', hs)] + WIN_ALL)
            mm_group(gb_, n, lambda kc: win[:, kc, 2048 + cg * 128:2048 + (cg + 1) * 128], lambda kc: hT[:, kc, 0:n], 8,
                     [('hT', hs)] + WIN_ALL)
            sg = B['sig'][cg % 2]
            P.op('act', lambda e, sg=sg, gb_=gb_: e.activation(out=sg[:, 0:n], in_=psf(gb_)[:, 0:n], func=AF.Sigmoid),
                 reads=[('ps', gb_)], writes=[('sig', cg % 2)])
            for (c0, ln, u0) in segs:
                P.op('dve', lambda e, sg=sg, c0=c0, ln=ln, u0=u0, cg=cg, ab=ab: e.tensor_tensor(
                    out=ub[:, cg, u0 + CH:u0 + CH + ln], in0=psf(ab)[:, c0:c0 + ln], in1=sg[:, c0:c0 + ln], op=ALU.mult),
                    reads=[('ps', ab), ('sig', cg % 2)], writes=[('ubm', us)])
                if u32bufs is not None:
                    ubuf = u32bufs[segs.index((c0, ln, u0))]
                    P.op('dve', lambda e, sg=sg, c0=c0, ln=ln, cg=cg, ubuf=ubuf, ab=ab: e.tensor_tensor(
                        out=ubuf[:, cg, 0:CH], in0=psf(ab)[:, c0 + ln - CH:c0 + ln], in1=sg[:, c0 + ln - CH:c0 + ln],
                        op=ALU.mult), reads=[('ps', ab), ('sig', cg % 2)], writes=[('uf32', c0)])
        for cg in range(4):
            conv_rot[0] ^= 1
            bank = 5 + conv_rot[0]
            for (c0, ln, u0) in segs:
                for k in range(CW):
                    idx = k * 4 + cg
                    P.op('pe', lambda e, idx=idx, k=k, c0=c0, ln=ln, u0=u0, cg=cg, bank=bank: e.matmul(
                        psf(bank)[:, c0:c0 + ln], lhsT=convD[:, idx * 128:(idx + 1) * 128], rhs=ub[:, cg, u0 + k:u0 + k + ln],
                        start=(k == 0), stop=(k == CW - 1)),
                        reads=[('ubm', us), ('ubh', us), 'convD'], writes=[('ps', bank)])
            P.op('act', lambda e, cg=cg, bank=bank: e.activation(out=B['cfp'][:, cg, 0:n], in_=psf(bank)[:, 0:n], func=AF.Identity,
                                                                 bias=cbv[:, cg:cg + 1]),
                 reads=[('ps', bank), 'pv'], writes=[('cfp', cg)])
            P.op('act', lambda e, cg=cg, bank=bank: e.activation(out=B['csq'][:, cg, 0:n], in_=psf(bank)[:, 0:n], func=AF.Square,
                                                                 bias=cbv[:, cg:cg + 1]),
                 reads=[('ps', bank), 'pv'], writes=[('csq', cg)])
            P.op('act', lambda e, cg=cg, bank=bank: e.activation(out=B['cb16'][:, cg, 0:n], in_=psf(bank)[:, 0:n], func=AF.Identity,
                                                                 bias=cbv[:, cg:cg + 1]),
                 reads=[('ps', bank), 'pv'], writes=[('cb16', cg)])
        if split:
            return lambda: p1_feature_tail(B, n, True)
        p1_feature_tail(B, n)

    def p1_feature_tail(B, n, split=False):
        mm_ = None
        dump('craw', B['cfp'][:, :, 0:n].rearrange("p a b -> p (a b)") if False else B['cfp'][:, 0, 0:n], [128, n], [('cfp', 0)])
        dump('convD', convD[:, 0:512], [128, 512], ['convD'], BF16)
        dump('pv', pv, [128, 160], ['pv'])
        for cg in range(4):
            P.op('pe', lambda e, cg=cg: e.matmul(psf(7)[:, 0:n], lhsT=onesB, rhs=B['cb16'][:, cg, 0:n],
                                                 start=(cg == 0), stop=(cg == 3)),
                 reads=[('cb16', cg), 'onesB'], writes=[('ps', 7)])
        for cg in range(4):
            P.op('pe', lambda e, cg=cg: e.matmul(psf(7)[:, GT:GT + n], lhsT=onesB, rhs=B['csq'][:, cg, 0:n],
                                                 start=(cg == 0), stop=(cg == 3)),
                 reads=[('csq', cg), 'onesB'], writes=[('ps', 7)])
        mm_, rs2, msq = B['mm'], B['rs2'], B['msq']
        P.op('dve', lambda e: e.tensor_scalar(out=mm_[:, 0:n], in0=psf(7)[:, 0:n], scalar1=1.0 / 512, scalar2=None, op0=ALU.mult),
             reads=[('ps', 7)], writes=['mm'])
        P.op('dve', lambda e: e.tensor_tensor(out=msq[:, 0:n], in0=mm_[:, 0:n], in1=mm_[:, 0:n], op=ALU.mult),
             reads=['mm'], writes=['msq'])
        P.op('dve', lambda e: e.scalar_tensor_tensor(out=rs2[:, 0:n], in0=psf(7)[:, GT:GT + n], scalar=1.0 / 512, in1=msq[:, 0:n],
                                                     op0=ALU.mult, op1=ALU.subtract),
             reads=[('ps', 7), 'msq'], writes=['rs2'])
        P.op('act', lambda e: e.activation(out=rs2[:, 0:n], in_=rs2[:, 0:n], func=AF.Sqrt, bias=epsT, scale=1.0),
             reads=['rs2', 'epsT'], writes=['rs2'])
        P.op('dve', lambda e: e.reciprocal(out=rs2[:, 0:n], in_=rs2[:, 0:n]), reads=['rs2'], writes=['rs2'])
        if split:
            return lambda: p1_feature_finish(B, n)
        p1_feature_finish(B, n)

    def p1_feature_finish(B, n):
        mm_, rs2 = B['mm'], B['rs2']
        for cg in range(4):
            sgt = B['sig'][cg % 2]
            P.op('dve', lambda e, cg=cg: e.tensor_tensor(out=B['cfp'][:, cg, 0:n], in0=B['cfp'][:, cg, 0:n], in1=mm_[:, 0:n],
                                                         op=ALU.subtract), reads=[('cfp', cg), 'mm'], writes=[('cfp', cg)])
            P.op('dve', lambda e, cg=cg: e.scalar_tensor_tensor(out=B['cfp'][:, cg, 0:n], in0=B['cfp'][:, cg, 0:n],
                                                                scalar=lgv[:, cg:cg + 1], in1=rs2[:, 0:n],
                                                                op0=ALU.mult, op1=ALU.mult),
                 reads=[('cfp', cg), 'rs2', 'pv'], writes=[('cfp', cg)])
            P.op('act', lambda e, cg=cg, sgt=sgt: e.activation(out=sgt[:, 0:n], in_=B['cfp'][:, cg, 0:n], func=AF.Sigmoid,
                                                               bias=lbv[:, cg:cg + 1]),
                 reads=[('cfp', cg), 'pv'], writes=[('sig', cg % 2)])
            P.op('dve', lambda e, cg=cg, sgt=sgt: e.scalar_tensor_tensor(out=B['cTs'][:, cg, 0:n], in0=B['cfp'][:, cg, 0:n],
                                                                         scalar=lbv[:, cg:cg + 1], in1=sgt[:, 0:n],
                                                                         op0=ALU.add, op1=ALU.mult),
                 reads=[('cfp', cg), ('sig', cg % 2), 'pv'], writes=['cTs'])

    def conv_out(B, dst_ap, key, ubuf=None):
        ubuf = B['uf32'] if ubuf is None else ubuf
        for cg in range(4):
            P.op('pe', lambda e, cg=cg: e.transpose(out=psf(0)[0:CH, cg * 128:(cg + 1) * 128], in_=ubuf[:, cg, 0:CH],
                                                    identity=identF),
                 reads=[('uf32', key), 'identF'], writes=[('ps', 0)])
        P.op('act', lambda e: e.activation(out=B['kst'][0][0:CH, :], in_=psf(0)[0:CH, :], func=AF.Copy),
             reads=[('ps', 0)], writes=[('kst', 0)])
        P.dma('sp', lambda e: e.dma_start(out=dst_ap, in_=B['kst'][0][0:CH, :]), reads=[('kst', 0)], key=('kst', 0))

    def p1_prompt(B, b):
        P.op('pool', lambda e: e.memset(B['ub'][0][:, :, 0:CH], 0.0), writes=[('ubh', 0)])
        nsb = T // GT

        def norms_pre(sb):
            for j in range(NB1):
                blk = sb * NB1 + j
                r0 = b * T + blk * 128
                norm_pre(B, blk % 2, xp[r0:r0 + 128, :], 128, 'x')

        def norms_post(sb):
            for j in range(NB1):
                blk = sb * NB1 + j
                norm_post(B, blk % 2, 128, sb % 2, j * 128)

        norms_pre(0)
        norms_post(0)
        for sb in range(nsb):
            hs = sb % 2; us = sb % 2; t0 = sb * GT
            hT = B['hT'][hs]
            last = (sb == nsb - 1)
            tail = p1_feature(B, hs, GT, us, [(0, GT, 0)], [B['uf32']] if last else None, split=True)
            tail = tail()
            if not last:
                norms_pre(sb + 1)
            for fg in range(4):
                bank = gbank()
                mm_group(bank, GT, lambda kc: win[:, kc, fg * 128:(fg + 1) * 128], lambda kc: hT[:, kc, :], 8,
                         [('hT', hs)] + WIN_ALL)
                evac(B['qTs'][:, fg, :], psf(bank)[:, 0:GT], [('ps', bank)], ['qTs'])
            P.dma('sp', lambda e, t0=t0: e.dma_start(out=qTd[b].rearrange("g p t -> p g t")[:, :, t0:t0 + GT], in_=B['qTs']),
                  reads=['qTs'], writes=[('qTd', sb)], key='qTs')
            for fg in range(4):
                bank = gbank()
                mm_group(bank, GT, lambda kc: win[:, kc, 512 + fg * 128:512 + (fg + 1) * 128], lambda kc: hT[:, kc, :], 8,
                         [('hT', hs)] + WIN_ALL)
                evac(kT[:, fg, t0:t0 + GT], psf(bank)[:, 0:GT], [('ps', bank)], [('kT', sb)])
            for j in range(NB1):
                blk = sb * NB1 + j; sl = blk % 2; tt = blk * 128
                bank = gbank()
                mm_group(bank, 512, lambda kc: hT[:, kc, j * 128:(j + 1) * 128], lambda kc: win[:, kc, 512:1024], 8,
                         [('hT', hs)] + WIN_ALL)
                evac(B['kst'][sl], psf(bank), [('ps', bank)], [('kst', sl)])
                P.dma('sp', lambda e, sl=sl, tt=tt: e.dma_start(
                    out=kp[b, :, tt:tt + 128, :].rearrange("h t d -> t h d"), in_=r3(B['kst'][sl], HD)),
                    reads=[('kst', sl)], key=('kst', sl))
                bank = gbank()
                mm_group(bank, 512, lambda kc: hT[:, kc, j * 128:(j + 1) * 128], lambda kc: win[:, kc, 1024:1536], 8,
                         [('hT', hs)] + WIN_ALL)
                evac(B['vst'][sl], psf(bank), [('ps', bank)], [('vst', sl)])
                P.op('pool', lambda e, sl=sl, blk=blk: e.tensor_copy(out=vv[:, blk, :], in_=B['vst'][sl]),
                     reads=[('vst', sl)], writes=[('vv', blk)])
                P.dma('sp', lambda e, sl=sl, tt=tt: e.dma_start(
                    out=vp[b, :, tt:tt + 128, :].rearrange("h t d -> t h d"), in_=r3(B['vst'][sl], HD)),
                    reads=[('vst', sl)], key=('vst', sl))
            if not last:
                norms_post(sb + 1)
            tail()
            P.dma('sp', lambda e, t0=t0: e.dma_start(out=cTd[b].rearrange("g p t -> p g t")[:, :, t0:t0 + GT], in_=B['cTs']),
                  reads=['cTs'], writes=[('cTd', sb)], key='cTs')
            if not last:
                P.op('pool', lambda e, us=us: e.tensor_copy(out=B['ub'][1 - us][:, :, 0:CH], in_=B['ub'][us][:, :, GT:GT + CH]),
                     reads=[('ubm', us)], writes=[('ubh', 1 - us)])
            else:
                conv_out(B, cp[b], 0)

    def p2_alloc():
        B = {}
        B['qe'] = [r3(A.bf(4 * 128), 128) for _ in range(2)]
        B['qo'] = [r3(A.bf(4 * 128), 128) for _ in range(2)]
        for i in range(2):
            P.op('pool', lambda e, i=i: e.memset(B['qe'][i][64:128, :, :], 0.0), writes=[('qblk', i)])
            P.op('pool', lambda e, i=i: e.memset(B['qo'][i][0:64, :, :], 0.0), writes=[('qblk', i)])
        B['cblk'] = [r3(A.bf(4 * 128), 128) for _ in range(2)]
        B['xres'] = [A.f32(D) for _ in range(2)]
        B['pbuf'] = [A.f32(514) for _ in range(3)]
        B['Cbuf'] = [A.f32(514) for _ in range(4)]
        B['wbuf'] = [A.bf(512) for _ in range(5)]
        B['wT'] = [r3(A.bf(512), 128) for _ in range(4)]
        B['osb'] = A.f32(512)
        B['onb'] = A.bf(512); B['sq'] = B['onb']; B['onT'] = r3(A.bf(512), 128)
        B['x2s'] = [A.f32(D)] * 2
        for i in range(3):
            P.op('pool', lambda e, i=i: e.memset(B['pbuf'][i][:, 512:513], 1.0), writes=[('pbuf', i)])
        P.op('dve', lambda e: e.memset(psf(6)[:, 0:16], 0.0), writes=['pszero'])
        return B

    class AttnPipe:
        def __init__(self, B):
            self.B = B
            self.items = []
            self.n = 0
            self.deferred = {}

        def add(self, item):
            self.items.append(item)

        def run(self):
            B = self.B
            n = len(self.items)
            for step in range(n + 12):
                if step < n:
                    self.s01(step)
                if 0 <= step - 1 < n:
                    self.s23(step - 1)
                if 0 <= step - 4 < n:
                    self.s45(step - 4)
                if 0 <= step - 6 < n:
                    self.s6(step - 6)
                for fn in self.deferred.pop(step, []):
                    fn()
            assert not self.deferred

        def defer(self, step, fn):
            self.deferred.setdefault(step, []).append(fn)

        def s01(self, i):
            it = self.items[i]
            if it.get('pre'):
                it['pre']()
            R, W = it['R'], it['W']
            zb = 1 + (i % 2)
            sl = i % 3
            qT = it['qT']; kTa = it['kT']
            rd = it['rd']
            if it['masked']:
                mw = it['mw']
                if W > mw:
                    P.op('pe', lambda e: e.matmul(psf(zb)[0:R, 0:W - mw], lhsT=qT, rhs=kTa[:, 0:W - mw], start=True, stop=True),
                         reads=rd, writes=[('ps', zb)])
                P.op('pe', lambda e: e.matmul(psf(zb)[0:R, W - mw:W], lhsT=qT, rhs=kTa[:, W - mw:W], start=True, stop=False),
                     reads=rd, writes=[('ps', zb)])
                P.op('pe', lambda e: e.matmul(psf(zb)[0:R, W - mw:W], lhsT=identB[0:R, 0:R], rhs=negB[0:R, 0:mw],
                                              start=False, stop=True),
                     reads=['identB', 'negB'], writes=[('ps', zb)])
            else:
                P.op('pe', lambda e: e.matmul(psf(zb)[0:R, 0:W], lhsT=qT, rhs=kTa, start=True, stop=True),
                     reads=rd, writes=[('ps', zb)])
            pb = B_ = self.B['pbuf'][sl]
            P.op('act', lambda e: e.activation(out=pb[0:R, 512 - W:512], in_=psf(zb)[0:R, 0:W], func=AF.Sigmoid, scale=-0.125),
                 reads=[('ps', zb)], writes=[('pbuf', sl)])

        def s23(self, i):
            it = self.items[i]
            R, W = it['R'], it['W']
            sl = i % 3
            s4 = i % 4
            s5 = i % 5
            pb = self.B['pbuf'][sl]; cb = self.B['Cbuf'][s4]; wb = self.B['wbuf'][s5]
            if it['first']:
                init = 1.0
                rds = [('pbuf', sl), 'pszero']
            else:
                pit = self.items[i - 1]
                psl = (i - 1) % 4
                init = self.B['Cbuf'][psl][0:R, 512 - pit['W']:513 - pit['W']]
                rds = [('pbuf', sl), 'pszero', ('Cbuf', psl)]
            P.op('dve', lambda e: e.tensor_tensor_scan(out=cb[0:R, 512 - W:513][:, ::-1], data0=pb[0:R, 512 - W:513][:, ::-1],
                                                       data1=psf(6)[0:R, 0:1].to_broadcast([R, W + 1]), initial=init,
                                                       op0=ALU.mult, op1=ALU.add),
                 reads=rds, writes=[('Cbuf', s4)])
            P.op('pool', lambda e: e.tensor_tensor(out=wb[0:R, 0:W], in0=cb[0:R, 513 - W:513], in1=cb[0:R, 512 - W:512],
                                                   op=ALU.subtract),
                 reads=[('Cbuf', s4)], writes=[('wbuf', s5)])

        def s45(self, i):
            it = self.items[i]
            R, W = it['R'], it['W']
            sl = i % 4
            s5 = i % 5
            wb = self.B['wbuf'][s5]; wT = self.B['wT'][sl]
            tb = 3 + (i % 2)
            tp = r3(psb(tb), 128)
            nkb = (W + 127) // 128
            for kb in range(nkb):
                kw = min(128, W - kb * 128)
                P.op('pe', lambda e, kb=kb, kw=kw: e.transpose(out=tp[0:kw, kb, 0:R], in_=wb[0:R, kb * 128:kb * 128 + kw],
                                                               identity=identB[0:R, 0:R]),
                     reads=[('wbuf', s5), 'identB'], writes=[('ps', tb)])
            kw0 = min(128, W)
            if True:
                P.op('act', lambda e: e.activation(out=wT[0:kw0, 0:nkb, 0:R], in_=tp[0:kw0, 0:nkb, 0:R], func=AF.Copy),
                     reads=[('ps', tb)], writes=[('wT', sl)])
            else:
                P.op('dve', lambda e: e.tensor_copy(out=wT[0:kw0, 0:nkb, 0:R], in_=tp[0:kw0, 0:nkb, 0:R]),
                     reads=[('ps', tb)], writes=[('wT', sl)])

        def s6(self, i):
            it = self.items[i]
            R, W = it['R'], it['W']
            sl = i % 4
            wT = self.B['wT'][sl]
            ob = it['obank']; h = it['h']
            nkb = (W + 127) // 128
            for kb in range(nkb):
                kw = min(128, W - kb * 128)
                vap = it['v'](kb)
                P.op('pe', lambda e, kb=kb, kw=kw, vap=vap: e.matmul(psf(ob)[0:R, h * HD:(h + 1) * HD], lhsT=wT[0:kw, kb, 0:R], rhs=vap,
                                                                     start=(it['first'] and kb == 0),
                                                                     stop=(it['last'] and kb == nkb - 1)),
                     reads=[('wT', sl)] + it['vrd'], writes=[('ps', ob)])
            if it.get('post'):
                it['post'](i + 6)

    def epilogue(pipe, B, R, ob, cT_fn, crd, xres_ap, xrd, dst_ap, uid):
        osb, sq, onb, onT = B['osb'], B['sq'], B['onb'], B['onT']
        x2s = B['x2s'][uid % 2]

        def e1():
            P.op('act', lambda e: e.activation(out=osb[0:R, :], in_=psf(ob)[0:R, :], func=AF.Copy),
                 reads=[('ps', ob)], writes=['osb'])
            P.op('dve', lambda e: e.tensor_tensor(out=sq[0:R, :], in0=osb[0:R, :], in1=osb[0:R, :], op=ALU.mult),
                 reads=['osb'], writes=['sq', 'onb'])
            P.op('dve', lambda e: e.tensor_reduce(out=sm[0:R, 8:16], in_=r3(sq, HD)[0:R], axis=AX.X, op=ALU.add),
                 reads=['sq', 'onb'], writes=['ss8'])
            rstd_pool(sm[0:R, 8:16], sm[0:R, 8:16], 1.0 / HD, 8, ['ss8'], ['ss8'])
            P.op('dve', lambda e: e.tensor_tensor(out=r3(onb, HD)[0:R], in0=r3(osb, HD)[0:R],
                                                  in1=sm[0:R, 8:16].unsqueeze(2).to_broadcast([R, NH, HD]), op=ALU.mult),
                 reads=['osb', 'ss8'], writes=['onb'])
            dump('osb', osb, [128, 512], ['osb'])
            dump('onb', onb, [128, 512], ['onb'], BF16)

        def e2():
            tp = r3(psb(7), 128)
            for j in range(4):
                P.op('pe', lambda e, j=j: e.transpose(out=tp[:, j, 0:R], in_=onb[0:R, j * 128:(j + 1) * 128],
                                                      identity=identB[0:R, 0:R]),
                     reads=['onb', 'identB'], writes=[('ps', 7)])
            P.op('act', lambda e: e.activation(out=onT[:, :, 0:R], in_=tp[:, 0:4, 0:R], func=AF.Copy),
                 reads=[('ps', 7)], writes=['onT'])
            for nh in range(2):
                bank = 3 + nh
                bank = 0 if nh == 0 else 7
                for j in range(8):
                    lhs = onT[:, j, 0:R] if j < 4 else cT_fn(j - 4)
                    P.op('pe', lambda e, j=j, lhs=lhs, bank=bank, nh=nh: e.matmul(
                        psf(bank)[0:R, :], lhsT=lhs, rhs=wout[:, j, nh * 512:(nh + 1) * 512], start=(j == 0), stop=(j == 7)),
                        reads=['onT', ('wout', j)] + crd, writes=[('ps', bank)])
                P.op('dve', lambda e, bank=bank, nh=nh: e.tensor_tensor(out=x2s[0:R, nh * 512:(nh + 1) * 512], in0=psf(bank)[0:R, :],
                                                                        in1=xres_ap[:, nh * 512:(nh + 1) * 512], op=ALU.add),
                     reads=[('ps', bank)] + xrd, writes=[('x2s', 0)])
            dump('onT', onT.rearrange("p a b -> p (a b)"), [128, 512], ['onT'], BF16)
            dump('x2s', x2s, [128, 1024], [('x2s', 0)])
            P.dma('sp', lambda e: e.dma_start(out=dst_ap, in_=x2s[0:R, :]), reads=[('x2s', 0)], key=('x2s', 0))
        return e1, e2

    def p2_prompt(B, b):
        pipe = AttnPipe(B)

        def loads(i):
            s = i % 2
            P.dma('sp', lambda e: e.dma_start(out=B['qe'][s][0:64, :, :],
                                              in_=qTd[b].rearrange("g p t -> p g t")[0:64, :, i * 128:(i + 1) * 128]),
                  writes=[('qblk', s)], key=('qblk', s))
            P.dma('sp', lambda e: e.dma_start(out=B['qo'][s][64:128, :, :],
                                              in_=qTd[b].rearrange("g p t -> p g t")[64:128, :, i * 128:(i + 1) * 128]),
                  writes=[('qblk', s)], key=('qblk', s))

        def loads_xc(i):
            s = i % 2
            P.dma('sp', lambda e: e.dma_start(out=B['cblk'][s], in_=cTd[b].rearrange("g p t -> p g t")[:, :, i * 128:(i + 1) * 128]),
                  writes=[('cblk', s)], key=('cblk', s))
            r0 = b * T + i * 128
            P.dma('sp', lambda e: e.dma_start(out=B['xres'][s], in_=xp[r0:r0 + 128, :]), writes=[('xres', s)], key=('xres', s))

        loads_xc(0)
        loads(0)
        for i in range(T // 128):
            s = i % 2
            ob = 5
            nk = (i + 1) * 128
            c_hi = (nk - 1) // 512
            for h in range(NH):
                hp, po = h // 2, (h % 2) * 64
                for ci, c in enumerate(range(c_hi, -1, -1)):
                    c0 = c * 512
                    W = min(512, nk - c0)
                    it = dict(R=128, W=W, h=h, obank=ob, first=(ci == 0), last=(c == 0), masked=(ci == 0), mw=128,
                              qT=(B['qe'] if h % 2 == 0 else B['qo'])[s][:, hp, :], kT=kT[:, hp, c0:c0 + W],
                              rd=[('qblk', s)] + [('kT', (c0 + x) // GT) for x in range(0, W, GT)],
                              v=(lambda kb, c0=c0, h=h: vv[:, c0 // 128 + kb, h * HD:(h + 1) * HD]),
                              vrd=[('vv', c0 // 128 + kb) for kb in range((W + 127) // 128)])
                    if h == 0 and ci == 0 and i + 1 < T // 128:
                        it['pre'] = (lambda i=i: loads(i + 1))
                    if h == NH - 1 and c == 0:
                        def post(step, i=i, s=s, ob=ob):
                            r0 = b * T + i * 128
                            e1, e2 = epilogue(pipe, B, 128, ob, lambda j: B['cblk'][s][:, j, :], [('cblk', s)],
                                              B['xres'][s], [('xres', s)], x2d[r0:r0 + 128, :], i)
                            e1()
                            if i + 1 < T // 128:
                                loads_xc(i + 1)
                            pipe.defer(step + 2, e2)
                        it['post'] = post
                    pipe.add(it)
        pipe.run()

    def p3():
        SB3 = 256
        wg = r3(A.bf(8 * DFF), DFF); wu = r3(A.bf(8 * DFF), DFF); wd = r3(A.bf(NF * D), D)
        fgb = A.f32(D)
        mark3 = A.top
        wst3 = [A.f32(DFF) for _ in range(4)]
        A.top = mark3
        for kc in range(8):
            load_cast(wg[:, kc, :], w_gate[kc * 128:(kc + 1) * 128, :], DFF, g2[:, kc:kc + 1], wst3, 'wst3')
            load_cast(wu[:, kc, :], w_up[kc * 128:(kc + 1) * 128, :], DFF, g2[:, kc:kc + 1], wst3, 'wst3')
        for f in range(NF):
            load_cast(wd[:, f, :], w_down[f * 128:(f + 1) * 128, :], D, None, wst3, 'wst3')
        P.dma('sp', lambda e: e.dma_start(out=fgb, in_=fg_d.partition_broadcast(128)), writes=['fgb'], key='k_fgb')
        P.barrier()
        x2b = [[A.f32(D) for _ in range(SB3 // 128)] for _ in range(2)]
        xn = [A.bf(D) for _ in range(2)]
        hfT = [r3(A.bf(8 * SB3), SB3) for _ in range(2)]
        act = r3(A.bf(NF * SB3), SB3)
        sg = [A.f32(SB3) for _ in range(2)]
        yf = [A.f32(D) for _ in range(2)]
        WG = []; WU = []

        blocks = []
        for i in range(2 * T // 128):
            blocks.append((i * 128, 128, yp[i * 128:(i + 1) * 128, :]))
        nb3 = SB3 // 128
        sbs = [blocks[i:i + nb3] for i in range(0, len(blocks), nb3)]
        sbs.append([(2 * T, 2 * ST, ys[:, :])])

        def cols_of(sbl):
            c = 0; out = []
            for (_, rows, _) in sbl:
                out.append(c); c += rows
            return out

        def norm_pre3(sbi):
            st_ = sbi % 2
            for j, (r0, rows, dst) in enumerate(sbs[sbi]):
                xb = x2b[st_][j]
                P.dma('pool', lambda e, xb=xb, r0=r0, rows=rows: e.dma_start(out=xb[0:rows, :], in_=x2d[r0:r0 + rows, :]),
                      writes=[('x2b', st_, j)], key=('x2b', st_, j))
                P.op('act', lambda e, xb=xb, rows=rows, j=j: e.activation(out=xn[j][0:rows, :], in_=xb[0:rows, :], func=AF.Square,
                                                                          accum_out=sm[0:rows, j:j + 1]),
                     reads=[('x2b', st_, j)], writes=[('xn', j), ('ss', j)])
                P.op('act', lambda e, rows=rows, j=j: e.activation(out=sm[0:rows, 2 + j:3 + j], in_=sm[0:rows, j:j + 1],
                                                                   func=AF.Sqrt, scale=1.0 / D, bias=epsT[0:rows, :]),
                     reads=[('ss', j), 'epsT'], writes=[('rs', j)])
                P.op('dve', lambda e, rows=rows, j=j: e.reciprocal(out=sm[0:rows, 2 + j:3 + j], in_=sm[0:rows, 2 + j:3 + j]),
                     reads=[('rs', j)], writes=[('rs', j)])
                P.op('dve', lambda e, xb=xb, rows=rows, j=j: e.tensor_scalar(out=xn[j][0:rows, :], in0=xb[0:rows, :],
                                                                             scalar1=sm[0:rows, 2 + j:3 + j], scalar2=None,
                                                                             op0=ALU.mult),
                     reads=[('x2b', st_, j), ('rs', j)], writes=[('xn', j)])

        def norm_post3(sbi):
            st_ = sbi % 2
            cols = cols_of(sbs[sbi])
            for j, (r0, rows, dst) in enumerate(sbs[sbi]):
                tp = r3(psb(0), 128)
                for kc in range(8):
                    P.op('pe', lambda e, kc=kc, rows=rows, j=j, tp=tp: e.transpose(out=tp[:, kc, 0:rows],
                                                                                  in_=xn[j][0:rows, kc * 128:(kc + 1) * 128],
                                                                                  identity=identB[0:rows, 0:rows]),
                         reads=[('xn', j), 'identB'], writes=[('ps', 0)])
                col = cols[j]
                P.op('act', lambda e, col=col, rows=rows, tp=tp: e.activation(out=hfT[st_][:, :, col:col + rows], in_=tp[:, :, 0:rows],
                                                                              func=AF.Copy),
                     reads=[('ps', 0)], writes=[('hfT', st_)])

        norm_pre3(0)
        norm_post3(0)
        for sbi, sbl in enumerate(sbs):
            st_ = sbi % 2
            n = sum(r for _, r, _ in sbl)
            cols = cols_of(sbl)
            hf = hfT[st_]
            if sbi + 1 < len(sbs):
                norm_pre3(sbi + 1)
            for f in range(NF):
                gb = 1 + 2 * (f % 2); ub_ = gb + 1
                for kc in range(8):
                    P.op('pe', lambda e, kc=kc, f=f, gb=gb, n=n, hf=hf: e.matmul(psf(gb)[:, 0:n], lhsT=wg[:, kc, f * 128:(f + 1) * 128],
                                                                                 rhs=hf[:, kc, 0:n], start=(kc == 0), stop=(kc == 7)),
                         reads=[('hfT', st_)] + WG, writes=[('ps', gb)])
                for kc in range(8):
                    P.op('pe', lambda e, kc=kc, f=f, ub_=ub_, n=n, hf=hf: e.matmul(psf(ub_)[:, 0:n], lhsT=wu[:, kc, f * 128:(f + 1) * 128],
                                                                                   rhs=hf[:, kc, 0:n], start=(kc == 0), stop=(kc == 7)),
                         reads=[('hfT', st_)] + WU, writes=[('ps', ub_)])
                P.op('act', lambda e, f=f, gb=gb, n=n: e.activation(out=sg[f % 2][:, 0:n], in_=psf(gb)[:, 0:n], func=AF.Silu),
                     reads=[('ps', gb)], writes=[('sg', f % 2)])
                P.op('dve', lambda e, f=f, ub_=ub_, n=n: e.tensor_tensor(out=act[:, f, 0:n], in0=psf(ub_)[:, 0:n], in1=sg[f % 2][:, 0:n],
                                                                         op=ALU.mult),
                     reads=[('ps', ub_), ('sg', f % 2)], writes=['act'])
            if sbi + 1 < len(sbs):
                norm_post3(sbi + 1)
            for j, (r0, rows, dst) in enumerate(sbl):
                xb = x2b[st_][j]; yo = yf[j % 2]; c0 = cols[j]
                for nh in range(2):
                    bank = 5 + nh
                    for f in range(NF):
                        P.op('pe', lambda e, f=f, nh=nh, bank=bank, rows=rows, c0=c0: e.matmul(
                            psf(bank)[0:rows, :], lhsT=act[:, f, c0:c0 + rows], rhs=wd[:, f, nh * 512:(nh + 1) * 512],
                            start=(f == 0), stop=(f == NF - 1)), reads=['act'], writes=[('ps', bank)])
                    P.op('dve', lambda e, nh=nh, bank=bank, rows=rows, xb=xb, yo=yo: e.tensor_tensor(
                        out=yo[0:rows, nh * 512:(nh + 1) * 512], in0=psf(bank)[0:rows, :], in1=xb[0:rows, nh * 512:(nh + 1) * 512],
                        op=ALU.add), reads=[('ps', bank), ('x2b', st_, j)], writes=[('yf', j % 2)])
                sl = 4 + (j % 2)
                P.op('act', lambda e, rows=rows, yo=yo, sl=sl, xb=xb: e.activation(out=xb[0:rows, :], in_=yo[0:rows, :],
                                                                                 func=AF.Square, accum_out=sm[0:rows, sl:sl + 1]),
                     reads=[('yf', j % 2)], writes=[('x2b', st_, j), ('ss', sl)])
                P.op('act', lambda e, rows=rows, sl=sl: e.activation(out=sm[0:rows, sl + 2:sl + 3], in_=sm[0:rows, sl:sl + 1],
                                                                     func=AF.Sqrt, scale=1.0 / D, bias=epsT[0:rows, :]),
                     reads=[('ss', sl), 'epsT'], writes=[('rs', sl)])
                P.op('dve', lambda e, rows=rows, sl=sl: e.reciprocal(out=sm[0:rows, sl + 2:sl + 3], in_=sm[0:rows, sl + 2:sl + 3]),
                     reads=[('rs', sl)], writes=[('rs', sl)])
                P.op('dve', lambda e, rows=rows, yo=yo, sl=sl: e.scalar_tensor_tensor(
                    out=yo[0:rows, :], in0=yo[0:rows, :], scalar=sm[0:rows, sl + 2:sl + 3], in1=fgb[0:rows, :],
                    op0=ALU.mult, op1=ALU.mult), reads=[('yf', j % 2), ('rs', sl), 'fgb'], writes=[('yf', j % 2)])
                P.dma('sp', lambda e, rows=rows, yo=yo, dst=dst: e.dma_start(out=dst, in_=yo[0:rows, :]),
                      reads=[('yf', j % 2)], key=('yf', j % 2))

    def p_sample(B1, S):
        n = 2 * ST
        B = B1
        norm_block(B, 0, xs[:, :], n, 0, 0, 'x')
        hT = B['hT'][0]
        for fg in range(4):
            bank = gbank()
            mm_group(bank, n, lambda kc: win[:, kc, fg * 128:(fg + 1) * 128], lambda kc: hT[:, kc, 0:n], 8, [('hT', 0)] + WIN_ALL)
            evac(S['qTe'][0:64, fg, :], psf(bank)[0:64, 0:n], [('ps', bank)], ['qTn'])
            evac(S['qTo'][64:128, fg, :], psf(bank)[64:128, 0:n], [('ps', bank)], ['qTn'])
        for fg in range(4):
            bank = gbank()
            mm_group(bank, n, lambda kc: win[:, kc, 512 + fg * 128:512 + (fg + 1) * 128], lambda kc: hT[:, kc, 0:n], 8,
                     [('hT', 0)] + WIN_ALL)
            evac(S['kTn'][:, fg, :], psf(bank)[:, 0:n], [('ps', bank)], ['kTn'])
        bank = gbank()
        mm_group(bank, 512, lambda kc: hT[:, kc, 0:n], lambda kc: win[:, kc, 512:1024], 8, [('hT', 0)] + WIN_ALL, m=n)
        evac(B['kst'][0][0:n, :], psf(bank)[0:n, :], [('ps', bank)], [('kst', 0)])
        for b in range(2):
            P.dma('sp', lambda e, b=b: e.dma_start(out=ks[b].rearrange("h t d -> t h d"),
                                                   in_=r3(B['kst'][0], HD)[b * ST:(b + 1) * ST]),
                  reads=[('kst', 0)], key=('kst', 0))
        for b in range(2):
            bank = gbank()
            mm_group(bank, 512, lambda kc: hT[:, kc, b * ST:(b + 1) * ST], lambda kc: win[:, kc, 1024:1536], 8,
                     [('hT', 0)] + WIN_ALL, m=ST)
            evac(B['vst'][b][0:ST, :], psf(bank)[0:ST, :], [('ps', bank)], [('vst', b)])
            P.op('pool', lambda e, b=b: e.tensor_copy(out=S['vn'][b][0:ST, :], in_=B['vst'][b][0:ST, :]),
                 reads=[('vst', b)], writes=[('vn', b)])
            P.dma('sp', lambda e, b=b: e.dma_start(out=vs[b].rearrange("h t d -> t h d"), in_=r3(B['vst'][b], HD)[0:ST]),
                  reads=[('vst', b)], key=('vst', b))
        P.dma('sp', lambda e: e.dma_start(out=S['sct'][0:2 * CH, :], in_=sc[:, :]), writes=['sct'], key='sct')
        ub = B['ub'][0]
        seg_w = CH + ST
        for cg in range(4):
            P.op('pe', lambda e, cg=cg: e.transpose(out=psf(0)[:, 0:2 * CH], in_=S['sct'][0:2 * CH, cg * 128:(cg + 1) * 128],
                                                    identity=identF[0:2 * CH, 0:2 * CH]),
                 reads=['sct', 'identF'], writes=[('ps', 0)])
            for b in range(2):
                P.op('act', lambda e, cg=cg, b=b: e.activation(out=ub[:, cg, b * seg_w:b * seg_w + CH],
                                                               in_=psf(0)[:, b * CH:(b + 1) * CH], func=AF.Copy),
                     reads=[('ps', 0)], writes=[('ubh', 0)])
        p1_feature(B, 0, n, 0, [(0, ST, 0), (ST, ST, seg_w)], [B['uf32'], S['uf32b']])
        for cg in range(4):
            P.op('pool', lambda e, cg=cg: e.tensor_copy(out=S['cTn'][:, cg, :], in_=B['cTs'][:, cg, 0:n]),
                 reads=['cTs'], writes=['cTn'])
        conv_out(B, cs[0], 0, B['uf32'])
        conv_out(B, cs[1], ST, S['uf32b'])

    GB = 8

    def cache_issue(b, g):
        stg = [r3(A.t[:, win_off // 2 + i * GB * 1024: win_off // 2 + (i + 1) * GB * 1024].bitcast(F32), 512) for i in range(4)]
        ks_ = stg[g % 2]; vs_ = stg[2 + g % 2]
        for h in range(NH):
            P.dma('sp', lambda e, h=h, g=g, ks_=ks_: e.dma_start(
                out=ks_[:, :, h * HD:(h + 1) * HD],
                in_=ck[b, h, g * GB * 128:(g + 1) * GB * 128, :].rearrange("(k p) d -> p k d", p=128)),
                writes=[('kstg', g % 2)], key=('kstg', g % 2))
        for h in range(NH):
            P.dma('sp', lambda e, h=h, g=g, vs_=vs_: e.dma_start(
                out=vs_[:, :, h * HD:(h + 1) * HD],
                in_=cv[b, h, g * GB * 128:(g + 1) * GB * 128, :].rearrange("(k p) d -> p k d", p=128)),
                writes=[('vstg', g % 2)], key=('vstg', g % 2))

    def cache_proc(g):
        stg = [r3(A.t[:, win_off // 2 + i * GB * 1024: win_off // 2 + (i + 1) * GB * 1024].bitcast(F32), 512) for i in range(4)]
        ks_ = stg[g % 2]; vs_ = stg[2 + g % 2]
        for j in range(GB):
            blk = g * GB + j
            tb = 3 + (blk % 2)
            for hp in range(4):
                P.op('pe', lambda e, j=j, hp=hp, tb=tb, ks_=ks_: e.transpose(out=psf(tb)[:, hp * 128:(hp + 1) * 128],
                                                                         in_=ks_[:, j, hp * 128:(hp + 1) * 128], identity=identF),
                     reads=[('kstg', g % 2), 'identF'], writes=[('ps', tb)])
            evac(kT[:, :, blk * 128:(blk + 1) * 128], r3(psf(tb), 128), [('ps', tb)], ['kTc'])
            evac(vv[:, blk, :], vs_[:, j, :], [('vstg', g % 2)], ['vvc'])

    def sample_attn(S, B, b, prefetched):
        for g in range(32 // GB):
            if g >= prefetched:
                cache_issue(b, g)
            if g + 1 < 32 // GB and g + 1 >= prefetched and False:
                pass
            cache_proc(g)
        P.dma('sp', lambda e: e.dma_start(out=B['xres'][b][0:ST, :], in_=xs[b * ST:(b + 1) * ST, :]),
              writes=[('xres', b)], key=('xres', b))
        P.barrier()
        if b == 0:
            cache_issue(1, 0)
            cache_issue(1, 1)
        pipe = AttnPipe(B)
        ob = 5
        for h in range(NH):
            hp, po = h // 2, (h % 2) * 64
            qTa = (S['qTe'] if h % 2 == 0 else S['qTo'])[:, hp, b * ST:(b + 1) * ST]
            it = dict(R=ST, W=ST, h=h, obank=ob, first=True, last=False, masked=True, mw=ST,
                      qT=qTa, kT=S['kTn'][:, hp, b * ST:(b + 1) * ST], rd=[],
                      v=(lambda kb, h=h: S['vn'][b][0:ST, h * HD:(h + 1) * HD]), vrd=[])
            pipe.add(it)
            for c in range(7, -1, -1):
                c0 = c * 512
                it = dict(R=ST, W=512, h=h, obank=ob, first=False, last=(c == 0), masked=False, mw=0,
                          qT=qTa, kT=kT[:, hp, c0:c0 + 512], rd=[],
                          v=(lambda kb, c0=c0, h=h: vv[:, c0 // 128 + kb, h * HD:(h + 1) * HD]), vrd=[])
                if h == NH - 1 and c == 0:
                    def post(step):
                        r0 = 2 * T + b * ST
                        e1, e2 = epilogue(pipe, B, ST, ob, lambda j: S['cTn'][:, j, b * ST:(b + 1) * ST], [],
                                          B['xres'][b][0:ST, :], [('xres', b)], x2d[r0:r0 + ST, :], b)
                        e1()
                        pipe.defer(step + 2, e2)
                    it['post'] = post
                pipe.add(it)
        pipe.run()
        P.barrier()

    for b in range(2):
        A.top = phase_top
        B1 = p1_alloc()
        p1_prompt(B1, b)
        P.barrier()
        A.top = phase_top
        B2 = p2_alloc()
        p2_prompt(B2, b)
        P.barrier()
    A.top = phase_top
    S = {}
    S['kTn'] = r3(A.bf(4 * 64), 64)
    S['qTe'] = r3(A.bf(4 * 64), 64)
    S['qTo'] = r3(A.bf(4 * 64), 64)
    P.op('pool', lambda e: e.memset(S['qTe'][64:128, :, :], 0.0), writes=['qTn'])
    P.op('pool', lambda e: e.memset(S['qTo'][0:64, :, :], 0.0), writes=['qTn'])
    S['cTn'] = r3(A.bf(4 * 64), 64)
    S['vn'] = [A.bf(512) for _ in range(2)]
    S['sct'] = A.f32(512)
    S['uf32b'] = r3(A.f32(4 * 32), 32)
    s_top = A.top
    B1 = p1_alloc(1)
    p_sample(B1, S)
    P.barrier()
    A.top = s_top
    B2 = p2_alloc()
    sample_attn(S, B2, 0, 0)
    sample_attn(S, B2, 1, 2)
    A.top = base_top
    p3()
    P.barrier()

    with nc.Block() as block:
        P.emit(block)
    stack.close()
    return nc


_CACHE = {}


def _consts():
    c = np.zeros((128, 384), np.float32)
    c[:, 0:128] = np.eye(128, dtype=np.float32)
    i = np.arange(128)
    c[:, 128:256] = np.where(i[None, :] >= i[:, None], NEG, 0.0).astype(np.float32)
    c[:, 256:384] = 1.0
    return c


def kernel(x_prompt, x_sample, cache_k, cache_v, state_conv, w_in, sb_norm_g, conv_w, conv_b, conv_ln_g,
           conv_ln_b, w_out, norm1_g, norm2_g, w_gate, w_up, w_down, final_g, _ncores=8):
    f = lambda a: np.ascontiguousarray(np.asarray(a, dtype=np.float32))
    nc = bass.Bass("TRN2", target_bir_lowering=False)
    build_program(nc)
    cst = _consts()
    shared = dict(w_in=f(w_in[0]), sbg=f(sb_norm_g[0]).reshape(512), convw=f(conv_w[0]), convb=f(conv_b[0]),
                  lng=f(conv_ln_g[0]), lnb=f(conv_ln_b[0]), w_out=f(w_out[0]), g1=f(norm1_g[0]), g2=f(norm2_g[0]),
                  w_gate=f(w_gate[0]), w_up=f(w_up[0]), w_down=f(w_down[0]), fg=f(final_g), cst=cst)
    in_maps = []
    for c in range(_ncores):
        m = dict(shared)
        m['xp'] = f(x_prompt[2 * c:2 * c + 2]).reshape(2 * T, D)
        m['xs'] = f(x_sample[2 * c:2 * c + 2]).reshape(2 * ST, D)
        m['ck'] = f(cache_k[0, 2 * c:2 * c + 2])
        m['cv'] = f(cache_v[0, 2 * c:2 * c + 2])
        m['sc'] = f(state_conv[0, 2 * c:2 * c + 2]).reshape(2 * CH, 512)
        in_maps.append(m)
    res = run_bass_kernel_spmd(nc, in_maps, core_ids=list(range(_ncores)))
    R = res.results
    if DEBUG:
        DBG['r0'] = R[0]
    cat = lambda k, shp: np.concatenate([np.asarray(r[k]).reshape(shp) for r in R], axis=0)
    y_prompt = cat('yp', (2, T, D))
    y_sample = cat('ys', (2, ST, D))
    k_prompt = cat('kp', (2, NH, T, HD))[None]
    v_prompt = cat('vp', (2, NH, T, HD))[None]
    conv_prompt = cat('cp', (2, CH, 512))[None]
    k_sample = cat('ks', (2, NH, ST, HD))[None]
    v_sample = cat('vs', (2, NH, ST, HD))[None]
    conv_sample = cat('cs', (2, CH, 512))[None]
    return (y_prompt, y_sample, k_prompt, v_prompt, conv_prompt, k_sample, v_sample, conv_sample)
```

```python
import numpy as np
from contextlib import ExitStack
import concourse.bass as bass
import concourse.mybir as mybir
from concourse.bass_utils import run_bass_kernel_spmd

F32 = mybir.dt.float32
BF16 = mybir.dt.bfloat16
AF = mybir.ActivationFunctionType
ALU = mybir.AluOpType
AX = mybir.AxisListType

T = 4096
D = 1024
NH = 8
HD = 64
DIN = 2560
DFF = 2816
NF = DFF // 128
CW = 31
CH = 30
EPS = 1e-6
GT = 256
NB1 = GT // 128
ST = 32
NEG = -30000.0
SAME_SYNC = True
PARANOID = False
DEBUG = False
DBG = {}
ARENA_BYTES = 212736


class Prog:
    def __init__(self, nc, stack):
        self.nc, self.stack = nc, stack
        self.engs = ['pe', 'act', 'dve', 'pool', 'sp']
        self.q = {e: [] for e in self.engs}
        self.sems = []
        self.cur = {}
        self.cnt = {}
        for e in self.engs:
            self._newsem(e)
        self.lastw = {}
        self.rd = {}
        self.waited = {e: {} for e in self.engs}
        self.dsem = {}

    def _alloc(self, name):
        h = self.stack.enter_context(self.nc.semaphore(name))
        self.sems.append(h)
        return len(self.sems) - 1

    def _newsem(self, e):
        self.cur[e] = self._alloc(f"s{e}{len(self.sems)}")
        self.cnt[e] = 0

    def _need(self, eng, tok, waits, war=False, force=False):
        s, v, te = tok
        if te == eng and not force:
            if eng == 'pe' or war or not SAME_SYNC:
                return
        if self.waited[eng].get(s, 0) >= v:
            return
        self.waited[eng][s] = v
        for i, (s2, v2) in enumerate(waits):
            if s2 == s:
                waits[i] = (s, max(v, v2))
                return
        waits.append((s, v))

    def _deps(self, eng, reads, writes):
        waits = []
        for r in reads:
            t = self.lastw.get(r)
            if t:
                self._need(eng, t, waits)
        for w in writes:
            t = self.lastw.get(w)
            if t:
                self._need(eng, t, waits)
            for t in self.rd.get(w, {}).values():
                self._need(eng, t, waits, war=True)
        return waits

    def _commit(self, tok, reads, writes):
        for r in reads:
            self.rd.setdefault(r, {})[tok[0]] = tok
        for w in writes:
            self.lastw[w] = tok
            self.rd[w] = {}

    def _paranoid(self, eng, waits):
        for e in self.engs:
            if self.cnt[e] > 0:
                self._need(eng, (self.cur[e], self.cnt[e], e), waits, force=(e != 'pe' or eng != 'pe'))
        for d in self.dsem.values():
            self._need(eng, (d[0], d[1], None), waits)

    def op(self, eng, fn, reads=(), writes=()):
        waits = self._deps(eng, reads, writes)
        if PARANOID:
            self._paranoid(eng, waits)
        if self.cnt[eng] >= 60000:
            self._newsem(eng)
        self.cnt[eng] += 1
        tok = (self.cur[eng], self.cnt[eng], eng)
        self._commit(tok, reads, writes)
        self.q[eng].append((waits, fn, self.cur[eng], 1))

    def dma(self, eng, fn, reads=(), writes=(), key=None):
        waits = self._deps(eng, reads, writes)
        if PARANOID:
            self._paranoid(eng, waits)
        if key not in self.dsem or self.dsem[key][1] >= 60000:
            self.dsem[key] = [self._alloc(f"d{len(self.sems)}"), 0]
        d = self.dsem[key]
        d[1] += 16
        tok = (d[0], d[1], None)
        self._commit(tok, reads, writes)
        self.q[eng].append((waits, fn, d[0], 16))

    def barrier(self):
        toks = [(self.cur[e], self.cnt[e], e) for e in self.engs if self.cnt[e] > 0]
        toks += [(d[0], d[1], None) for d in self.dsem.values()]
        for e in self.engs:
            waits = []
            for t in toks:
                self._need(e, t, waits, force=True)
            self.q[e].append((waits, None, None, 0))
        self.lastw.clear()
        self.rd.clear()

    def emit(self, block):
        engmap = {'pe': block.tensor, 'act': block.scalar, 'dve': block.vector,
                  'pool': block.gpsimd, 'sp': block.sync}
        for e in self.engs:
            items = self.q[e]

            def body(eng, items=items):
                for waits, fn, sem, inc in items:
                    for s, v in waits:
                        eng.wait_ge(self.sems[s], v)
                    if fn is not None:
                        fn(eng).then_inc(self.sems[sem], inc)
            engmap[e](body)


class Arena:
    def __init__(self, nc, stack, nbytes):
        self.t = stack.enter_context(nc.sbuf_tensor("arena", [128, nbytes // 2], BF16))
        self.size = nbytes
        self.top = 0

    def alloc(self, nbytes):
        off = self.top
        self.top += (nbytes + 63) // 64 * 64
        assert self.top <= self.size, f"SBUF arena overflow {self.top} > {self.size}"
        return off

    def bf(self, n):
        off = self.alloc(n * 2)
        return self.t[:, off // 2: off // 2 + n]

    def f32(self, n):
        off = self.alloc(n * 4)
        return self.t[:, off // 2: off // 2 + 2 * n].bitcast(F32)


def r3(ap, b):
    return ap.rearrange("p (a b) -> p a b", b=b)


def build_program(nc):
    stack = ExitStack()
    P = Prog(nc, stack)
    A = Arena(nc, stack, ARENA_BYTES)

    def din(name, shape):
        return nc.dram_tensor(name, list(shape), F32, kind="ExternalInput").ap()

    def dout(name, shape):
        return nc.dram_tensor(name, list(shape), F32, kind="ExternalOutput").ap()

    xp = din("xp", [2 * T, D]); xs = din("xs", [2 * ST, D])
    ck = din("ck", [2, NH, T, HD]); cv = din("cv", [2, NH, T, HD])
    sc = din("sc", [2 * CH, 512])
    w_in = din("w_in", [D, DIN]); sbg_d = din("sbg", [512]); cw_d = din("convw", [CW, 512])
    cb_d = din("convb", [512]); lg_d = din("lng", [512]); lb_d = din("lnb", [512])
    w_out = din("w_out", [D, D]); g1_d = din("g1", [D]); g2_d = din("g2", [D])
    w_gate = din("w_gate", [D, DFF]); w_up = din("w_up", [D, DFF]); w_down = din("w_down", [DFF, D])
    fg_d = din("fg", [D]); cst_d = din("cst", [128, 384])
    yp = dout("yp", [2 * T, D]); ys = dout("ys", [2 * ST, D])
    kp = dout("kp", [2, NH, T, HD]); vp = dout("vp", [2, NH, T, HD]); cp = dout("cp", [2, CH, 512])
    ks = dout("ks", [2, NH, ST, HD]); vs = dout("vs", [2, NH, ST, HD]); cs = dout("cs", [2, CH, 512])
    dk = dict(kind="ExternalOutput") if DEBUG else {}
    x2d = nc.dram_tensor("x2d", [2 * T + 2 * ST, D], F32, **dk).ap()
    qTd = nc.dram_tensor("qTd", [2, 4, 128, T], BF16, **dk).ap()
    cTd = nc.dram_tensor("cTd", [2, 4, 128, T], BF16, **dk).ap()

    ps = [stack.enter_context(nc.psum_tensor(f"ps{i}", [128, 512], F32)) for i in range(8)]

    def psf(i):
        return ps[i][:, :]

    def psb(i):
        return ps[i][:, :].bitcast(BF16)

    dbg_n = [0]

    def dump(name, ap, shape, reads, dt=F32):
        if not DEBUG or name in DBG.get('_done', set()):
            return
        DBG.setdefault('_done', set()).add(name)
        t = nc.dram_tensor("dbg_" + name, list(shape), dt, kind="ExternalOutput").ap()
        P.dma('sp', lambda e: e.dma_start(out=t, in_=ap), reads=reads, key=('dbg', name))

    identF = A.f32(128); identB = A.bf(128); negB = A.bf(128); onesB = A.bf(128)
    epsT = A.f32(1)
    pv = A.f32(160)
    pvA = A.f32(128); pvC = A.f32(128)
    g1 = pv[:, 0:8]; g2 = pv[:, 8:16]; sbg = pv[:, 16:20]; cbv = pv[:, 20:24]
    lgv = pv[:, 24:28]; lbv = pv[:, 28:32]
    sm = A.f32(64)
    mh = A.f32(8)
    base_top = A.top

    P.dma('sp', lambda e: e.dma_start(out=identF, in_=cst_d[:, 0:128]), writes=['identF'], key='k_identF')
    P.dma('pool', lambda e: e.dma_start(out=identB, in_=cst_d[:, 0:128]), writes=['identB'], key='k_identB')
    P.dma('pool', lambda e: e.dma_start(out=negB, in_=cst_d[:, 128:256]), writes=['negB'], key='k_negB')
    P.dma('pool', lambda e: e.dma_start(out=onesB, in_=cst_d[:, 256:384]), writes=['onesB'], key='k_onesB')
    P.op('pool', lambda e: e.memset(epsT, EPS), writes=['epsT'])
    P.op('pool', lambda e: e.memset(mh, -0.5), writes=['mh'])

    def rstd_pool(dst, src, scale, w, rd, wr):
        P.op('pool', lambda e: e.tensor_scalar(out=dst, in0=src, scalar1=scale, scalar2=EPS, op0=ALU.mult, op1=ALU.add),
             reads=rd, writes=wr)
        P.op('pool', lambda e: e.tensor_tensor(out=dst, in0=dst, in1=mh[0:dst.shape[0], 0:w], op=ALU.pow),
             reads=wr + ['mh'], writes=wr)
    for i, v in enumerate([g1_d, g2_d]):
        P.dma('sp', lambda e, i=i, v=v: e.dma_start(out=pvA[8 * i:8 * i + 8, :], in_=v.rearrange("(k p) -> k p", p=128)),
              writes=['pvA'], key='k_pvA')
    for i, v in enumerate([sbg_d, cb_d, lg_d, lb_d]):
        P.dma('sp', lambda e, i=i, v=v: e.dma_start(out=pvA[16 + 4 * i:20 + 4 * i, :], in_=v.rearrange("(k p) -> k p", p=128)),
              writes=['pvA'], key='k_pvA')
    P.dma('sp', lambda e: e.dma_start(out=pvC[0:124, :], in_=cw_d.rearrange("k (g p) -> (k g) p", p=128)),
          writes=['pvC'], key='k_pvC')
    P.op('pe', lambda e: e.transpose(out=psf(0)[:, 0:32], in_=pvA[0:32, :], identity=identF[0:32, 0:32]),
         reads=['pvA', 'identF'], writes=[('ps', 0)])
    P.op('pe', lambda e: e.transpose(out=psf(0)[:, 32:156], in_=pvC[0:124, :], identity=identF[0:124, 0:124]),
         reads=['pvC', 'identF'], writes=[('ps', 0)])
    P.op('dve', lambda e: e.tensor_copy(out=pv[:, 0:156], in_=psf(0)[:, 0:156]), reads=[('ps', 0)], writes=['pv'])

    kT = r3(A.bf(4 * T), T)
    vv = r3(A.bf(32 * 512), 512)
    win_off = A.top
    win = r3(A.bf(8 * DIN), DIN)
    convD = A.bf(124 * 128)
    wout = r3(A.bf(8 * D), D)
    phase_top = A.top

    wst = [A.f32(DIN) for _ in range(4)]
    A.top = phase_top
    widx = [0]

    def load_cast(dst, src, n, scale_ap, stgs, tag):
        i = widx[0]; widx[0] += 1
        st_ = stgs[i % len(stgs)]
        sk = (tag, i % len(stgs))
        P.dma('sp', lambda e: e.dma_start(out=st_[:, 0:n], in_=src), writes=[sk], key=sk)
        if i % 2 == 0:
            if scale_ap is None:
                P.op('act', lambda e: e.activation(out=dst, in_=st_[:, 0:n], func=AF.Identity), reads=[sk], writes=[])
            else:
                P.op('act', lambda e: e.activation(out=dst, in_=st_[:, 0:n], func=AF.Identity, scale=scale_ap),
                     reads=[sk, 'pv'], writes=[])
        else:
            if scale_ap is None:
                P.op('dve', lambda e: e.tensor_copy(out=dst, in_=st_[:, 0:n]), reads=[sk], writes=[])
            else:
                P.op('dve', lambda e: e.tensor_scalar(out=dst, in0=st_[:, 0:n], scalar1=scale_ap, scalar2=None, op0=ALU.mult),
                     reads=[sk, 'pv'], writes=[])

    for kc in range(8):
        load_cast(win[:, kc, :], w_in[kc * 128:(kc + 1) * 128, :], DIN, g1[:, kc:kc + 1], wst, 'wst')
    for kc in range(8):
        load_cast(wout[:, kc, :], w_out[kc * 128:(kc + 1) * 128, :], D, sbg[:, kc:kc + 1] if kc < 4 else None, wst, 'wst')
    for k in range(CW):
        for g in range(4):
            idx = k * 4 + g
            eng = 'dve' if idx % 2 == 0 else 'pool'
            P.op(eng, lambda e, idx=idx: e.tensor_scalar(out=convD[:, idx * 128:(idx + 1) * 128], in0=identF,
                                                         scalar1=pv[:, 32 + idx:33 + idx], scalar2=None, op0=ALU.mult),
                 reads=['pv', 'identF'], writes=['convD'])
    P.barrier()

    def p1_alloc(nx=2):
        B = {}
        B['xblk'] = [A.f32(D) for _ in range(nx)]
        B['xn'] = [A.bf(D) for _ in range(nx)]
        B['hT'] = [r3(A.bf(8 * GT), GT) for _ in range(2)]
        B['qTs'] = r3(A.bf(4 * GT), GT)
        B['cTs'] = r3(A.bf(4 * GT), GT)
        B['kst'] = [A.f32(512) for _ in range(2)]
        B['vst'] = [A.f32(512) for _ in range(2)]
        B['sig'] = [A.f32(GT) for _ in range(2)]
        B['ub'] = [r3(A.bf(4 * (CH + GT + 2)), CH + GT + 2) for _ in range(2)]
        B['cfp'] = r3(A.f32(4 * GT), GT)
        B['cb16'] = r3(A.bf(4 * GT), GT)
        B['csq'] = r3(A.bf(4 * GT), GT)
        B['mm'] = A.f32(GT); B['rs2'] = A.f32(GT); B['msq'] = A.f32(GT)
        B['uf32'] = r3(A.f32(4 * 32), 32)
        return B

    gen_rot = [0]

    def gbank():
        gen_rot[0] = (gen_rot[0] + 1) % 4
        return 1 + gen_rot[0]

    conv_rot = [0]

    def norm_pre(B, slot, src_ap, rows, xkey):
        xb = B['xblk'][slot]; xn = B['xn'][slot]
        P.dma('pool', lambda e: e.dma_start(out=xb[0:rows, :], in_=src_ap), writes=[('xblk', slot)], key=(xkey, slot))
        P.op('act', lambda e: e.activation(out=xn[0:rows, :], in_=xb[0:rows, :], func=AF.Square,
                                           accum_out=sm[0:rows, slot:slot + 1]),
             reads=[('xblk', slot)], writes=[('xn', slot), ('ss', slot)])
        P.op('act', lambda e: e.activation(out=sm[0:rows, 2 + slot:3 + slot], in_=sm[0:rows, slot:slot + 1], func=AF.Sqrt,
                                           scale=1.0 / D, bias=epsT[0:rows, :]),
             reads=[('ss', slot), 'epsT'], writes=[('rs', slot)])
        P.op('dve', lambda e: e.reciprocal(out=sm[0:rows, 2 + slot:3 + slot], in_=sm[0:rows, 2 + slot:3 + slot]),
             reads=[('rs', slot)], writes=[('rs', slot)])
        P.op('dve', lambda e: e.tensor_scalar(out=xn[0:rows, :], in0=xb[0:rows, :], scalar1=sm[0:rows, 2 + slot:3 + slot],
                                              scalar2=None, op0=ALU.mult),
             reads=[('xblk', slot), ('rs', slot)], writes=[('xn', slot)])

    def norm_post(B, slot, rows, hs, col0):
        xn = B['xn'][slot]; hT = B['hT'][hs]
        tp = r3(psb(0), 128)
        for kc in range(8):
            P.op('pe', lambda e, kc=kc: e.transpose(out=tp[:, kc, 0:rows], in_=xn[0:rows, kc * 128:(kc + 1) * 128],
                                                    identity=identB[0:rows, 0:rows]),
                 reads=[('xn', slot), 'identB'], writes=[('ps', 0)])
        P.op('act', lambda e: e.activation(out=hT[:, :, col0:col0 + rows], in_=tp[:, :, 0:rows], func=AF.Copy),
             reads=[('ps', 0)], writes=[('hT', hs)])

    def norm_block(B, slot, src_ap, rows, hs, col0, xkey):
        norm_pre(B, slot, src_ap, rows, xkey)
        norm_post(B, slot, rows, hs, col0)

    def mm_group(bank, n, lhs_fn, rhs_fn, nk, reads, m=128):
        for kc in range(nk):
            lhs = lhs_fn(kc); rhs = rhs_fn(kc)
            P.op('pe', lambda e, kc=kc, lhs=lhs, rhs=rhs: e.matmul(psf(bank)[0:m, 0:n], lhsT=lhs, rhs=rhs,
                                                                   start=(kc == 0), stop=(kc == nk - 1)),
                 reads=reads, writes=[('ps', bank)])

    WIN_ALL = [('win', kc) for kc in range(8)]
    evac_rot = [0]

    def evac(out_ap, in_ap, reads, writes):
        evac_rot[0] ^= 1
        if evac_rot[0]:
            P.op('act', lambda e: e.activation(out=out_ap, in_=in_ap, func=AF.Copy), reads=reads, writes=writes)
        else:
            P.op('dve', lambda e: e.tensor_copy(out=out_ap, in_=in_ap), reads=reads, writes=writes)

    def p1_feature(B, hs, n, us, segs, u32bufs, split=False):
        hT = B['hT'][hs]; ub = B['ub'][us]
        for cg in range(4):
            ab, gb_ = (3, 4) if cg % 2 == 0 else (1, 2)
            mm_group(ab, n, lambda kc: win[:, kc, 1536 + cg * 128:1536 + (cg + 1) * 128], lambda kc: hT[:, kc, 0:n], 8,
                     [('hT', hs)] + WIN_ALL)
            mm_group(gb_, n, lambda kc: win[:, kc, 2048 + cg * 128:2048 + (cg + 1) * 128], lambda kc: hT[:, kc, 0:n], 8,
                     [('hT', hs)] + WIN_ALL)
            sg = B['sig'][cg % 2]
            P.op('act', lambda e, sg=sg, gb_=gb_: e.activation(out=sg[:, 0:n], in_=psf(gb_)[:, 0:n], func=AF.Sigmoid),
                 reads=[('ps', gb_)], writes=[('sig', cg % 2)])
            for (c0, ln, u0) in segs:
                P.op('dve', lambda e, sg=sg, c0=c0, ln=ln, u0=u0, cg=cg, ab=ab: e.tensor_tensor(
                    out=ub[:, cg, u0 + CH:u0 + CH + ln], in0=psf(ab)[:, c0:c0 + ln], in1=sg[:, c0:c0 + ln], op=ALU.mult),
                    reads=[('ps', ab), ('sig', cg % 2)], writes=[('ubm', us)])
                if u32bufs is not None:
                    ubuf = u32bufs[segs.index((c0, ln, u0))]
                    P.op('dve', lambda e, sg=sg, c0=c0, ln=ln, cg=cg, ubuf=ubuf, ab=ab: e.tensor_tensor(
                        out=ubuf[:, cg, 0:CH], in0=psf(ab)[:, c0 + ln - CH:c0 + ln], in1=sg[:, c0 + ln - CH:c0 + ln],
                        op=ALU.mult), reads=[('ps', ab), ('sig', cg % 2)], writes=[('uf32', c0)])
        for cg in range(4):
            conv_rot[0] ^= 1
            bank = 5 + conv_rot[0]
            for (c0, ln, u0) in segs:
                for k in range(CW):
                    idx = k * 4 + cg
                    P.op('pe', lambda e, idx=idx, k=k, c0=c0, ln=ln, u0=u0, cg=cg, bank=bank: e.matmul(
                        psf(bank)[:, c0:c0 + ln], lhsT=convD[:, idx * 128:(idx + 1) * 128], rhs=ub[:, cg, u0 + k:u0 + k + ln],
                        start=(k == 0), stop=(k == CW - 1)),
                        reads=[('ubm', us), ('ubh', us), 'convD'], writes=[('ps', bank)])
            P.op('act', lambda e, cg=cg, bank=bank: e.activation(out=B['cfp'][:, cg, 0:n], in_=psf(bank)[:, 0:n], func=AF.Identity,
                                                                 bias=cbv[:, cg:cg + 1]),
                 reads=[('ps', bank), 'pv'], writes=[('cfp', cg)])
            P.op('act', lambda e, cg=cg, bank=bank: e.activation(out=B['csq'][:, cg, 0:n], in_=psf(bank)[:, 0:n], func=AF.Square,
                                                                 bias=cbv[:, cg:cg + 1]),
                 reads=[('ps', bank), 'pv'], writes=[('csq', cg)])
            P.op('act', lambda e, cg=cg, bank=bank: e.activation(out=B['cb16'][:, cg, 0:n], in_=psf(bank)[:, 0:n], func=AF.Identity,
                                                                 bias=cbv[:, cg:cg + 1]),
                 reads=[('ps', bank), 'pv'], writes=[('cb16', cg)])
        if split:
            return lambda: p1_feature_tail(B, n, True)
        p1_feature_tail(B, n)

    def p1_feature_tail(B, n, split=False):
        mm_ = None
        dump('craw', B['cfp'][:, :, 0:n].rearrange("p a b -> p (a b)") if False else B['cfp'][:, 0, 0:n], [128, n], [('cfp', 0)])
        dump('convD', convD[:, 0:512], [128, 512], ['convD'], BF16)
        dump('pv', pv, [128, 160], ['pv'])
        for cg in range(4):
            P.op('pe', lambda e, cg=cg: e.matmul(psf(7)[:, 0:n], lhsT=onesB, rhs=B['cb16'][:, cg, 0:n],
                                                 start=(cg == 0), stop=(cg == 3)),
                 reads=[('cb16', cg), 'onesB'], writes=[('ps', 7)])
        for cg in range(4):
            P.op('pe', lambda e, cg=cg: e.matmul(psf(7)[:, GT:GT + n], lhsT=onesB, rhs=B['csq'][:, cg, 0:n],
                                                 start=(cg == 0), stop=(cg == 3)),
                 reads=[('csq', cg), 'onesB'], writes=[('ps', 7)])
        mm_, rs2, msq = B['mm'], B['rs2'], B['msq']
        P.op('dve', lambda e: e.tensor_scalar(out=mm_[:, 0:n], in0=psf(7)[:, 0:n], scalar1=1.0 / 512, scalar2=None, op0=ALU.mult),
             reads=[('ps', 7)], writes=['mm'])
        P.op('dve', lambda e: e.tensor_tensor(out=msq[:, 0:n], in0=mm_[:, 0:n], in1=mm_[:, 0:n], op=ALU.mult),
             reads=['mm'], writes=['msq'])
        P.op('dve', lambda e: e.scalar_tensor_tensor(out=rs2[:, 0:n], in0=psf(7)[:, GT:GT + n], scalar=1.0 / 512, in1=msq[:, 0:n],
                                                     op0=ALU.mult, op1=ALU.subtract),
             reads=[('ps', 7), 'msq'], writes=['rs2'])
        P.op('act', lambda e: e.activation(out=rs2[:, 0:n], in_=rs2[:, 0:n], func=AF.Sqrt, bias=epsT, scale=1.0),
             reads=['rs2', 'epsT'], writes=['rs2'])
        P.op('dve', lambda e: e.reciprocal(out=rs2[:, 0:n], in_=rs2[:, 0:n]), reads=['rs2'], writes=['rs2'])
        if split:
            return lambda: p1_feature_finish(B, n)
        p1_feature_finish(B, n)

    def p1_feature_finish(B, n):
        mm_, rs2 = B['mm'], B['rs2']
        for cg in range(4):
            sgt = B['sig'][cg % 2]
            P.op('dve', lambda e, cg=cg: e.tensor_tensor(out=B['cfp'][:, cg, 0:n], in0=B['cfp'][:, cg, 0:n], in1=mm_[:, 0:n],
                                                         op=ALU.subtract), reads=[('cfp', cg), 'mm'], writes=[('cfp', cg)])
            P.op('dve', lambda e, cg=cg: e.scalar_tensor_tensor(out=B['cfp'][:, cg, 0:n], in0=B['cfp'][:, cg, 0:n],
                                                                scalar=lgv[:, cg:cg + 1], in1=rs2[:, 0:n],
                                                                op0=ALU.mult, op1=ALU.mult),
                 reads=[('cfp', cg), 'rs2', 'pv'], writes=[('cfp', cg)])
            P.op('act', lambda e, cg=cg, sgt=sgt: e.activation(out=sgt[:, 0:n], in_=B['cfp'][:, cg, 0:n], func=AF.Sigmoid,
                                                               bias=lbv[:, cg:cg + 1]),
                 reads=[('cfp', cg), 'pv'], writes=[('sig', cg % 2)])
            P.op('dve', lambda e, cg=cg, sgt=sgt: e.scalar_tensor_tensor(out=B['cTs'][:, cg, 0:n], in0=B['cfp'][:, cg, 0:n],
                                                                         scalar=lbv[:, cg:cg + 1], in1=sgt[:, 0:n],
                                                                         op0=ALU.add, op1=ALU.mult),
                 reads=[('cfp', cg), ('sig', cg % 2), 'pv'], writes=['cTs'])

    def conv_out(B, dst_ap, key, ubuf=None):
        ubuf = B['uf32'] if ubuf is None else ubuf
        for cg in range(4):
            P.op('pe', lambda e, cg=cg: e.transpose(out=psf(0)[0:CH, cg * 128:(cg + 1) * 128], in_=ubuf[:, cg, 0:CH],
                                                    identity=identF),
                 reads=[('uf32', key), 'identF'], writes=[('ps', 0)])
        P.op('act', lambda e: e.activation(out=B['kst'][0][0:CH, :], in_=psf(0)[0:CH, :], func=AF.Copy),
             reads=[('ps', 0)], writes=[('kst', 0)])
        P.dma('sp', lambda e: e.dma_start(out=dst_ap, in_=B['kst'][0][0:CH, :]), reads=[('kst', 0)], key=('kst', 0))

    def p1_prompt(B, b):
        P.op('pool', lambda e: e.memset(B['ub'][0][:, :, 0:CH], 0.0), writes=[('ubh', 0)])
        nsb = T // GT

        def norms_pre(sb):
            for j in range(NB1):
                blk = sb * NB1 + j
                r0 = b * T + blk * 128
                norm_pre(B, blk % 2, xp[r0:r0 + 128, :], 128, 'x')

        def norms_post(sb):
            for j in range(NB1):
                blk = sb * NB1 + j
                norm_post(B, blk % 2, 128, sb % 2, j * 128)

        norms_pre(0)
        norms_post(0)
        for sb in range(nsb):
            hs = sb % 2; us = sb % 2; t0 = sb * GT
            hT = B['hT'][hs]
            last = (sb == nsb - 1)
            tail = p1_feature(B, hs, GT, us, [(0, GT, 0)], [B['uf32']] if last else None, split=True)
            tail = tail()
            if not last:
                norms_pre(sb + 1)
            for fg in range(4):
                bank = gbank()
                mm_group(bank, GT, lambda kc: win[:, kc, fg * 128:(fg + 1) * 128], lambda kc: hT[:, kc, :], 8,
                         [('hT', hs)] + WIN_ALL)
                evac(B['qTs'][:, fg, :], psf(bank)[:, 0:GT], [('ps', bank)], ['qTs'])
            P.dma('sp', lambda e, t0=t0: e.dma_start(out=qTd[b].rearrange("g p t -> p g t")[:, :, t0:t0 + GT], in_=B['qTs']),
                  reads=['qTs'], writes=[('qTd', sb)], key='qTs')
            for fg in range(4):
                bank = gbank()
                mm_group(bank, GT, lambda kc: win[:, kc, 512 + fg * 128:512 + (fg + 1) * 128], lambda kc: hT[:, kc, :], 8,
                         [('hT', hs)] + WIN_ALL)
                evac(kT[:, fg, t0:t0 + GT], psf(bank)[:, 0:GT], [('ps', bank)], [('kT', sb)])
            for j in range(NB1):
                blk = sb * NB1 + j; sl = blk % 2; tt = blk * 128
                bank = gbank()
                mm_group(bank, 512, lambda kc: hT[:, kc, j * 128:(j + 1) * 128], lambda kc: win[:, kc, 512:1024], 8,
                         [('hT', hs)] + WIN_ALL)
                evac(B['kst'][sl], psf(bank), [('ps', bank)], [('kst', sl)])
                P.dma('sp', lambda e, sl=sl, tt=tt: e.dma_start(
                    out=kp[b, :, tt:tt + 128, :].rearrange("h t d -> t h d"), in_=r3(B['kst'][sl], HD)),
                    reads=[('kst', sl)], key=('kst', sl))
                bank = gbank()
                mm_group(bank, 512, lambda kc: hT[:, kc, j * 128:(j + 1) * 128], lambda kc: win[:, kc, 1024:1536], 8,
                         [('hT', hs)] + WIN_ALL)
                evac(B['vst'][sl], psf(bank), [('ps', bank)], [('vst', sl)])
                P.op('pool', lambda e, sl=sl, blk=blk: e.tensor_copy(out=vv[:, blk, :], in_=B['vst'][sl]),
                     reads=[('vst', sl)], writes=[('vv', blk)])
                P.dma('sp', lambda e, sl=sl, tt=tt: e.dma_start(
                    out=vp[b, :, tt:tt + 128, :].rearrange("h t d -> t h d"), in_=r3(B['vst'][sl], HD)),
                    reads=[('vst', sl)], key=('vst', sl))
            if not last:
                norms_post(sb + 1)
            tail()
            P.dma('sp', lambda e, t0=t0: e.dma_start(out=cTd[b].rearrange("g p t -> p g t")[:, :, t0:t0 + GT], in_=B['cTs']),
                  reads=['cTs'], writes=[('cTd', sb)], key='cTs')
            if not last:
                P.op('pool', lambda e, us=us: e.tensor_copy(out=B['ub'][1 - us][:, :, 0:CH], in_=B['ub'][us][:, :, GT:GT + CH]),
                     reads=[('ubm', us)], writes=[('ubh', 1 - us)])
            else:
                conv_out(B, cp[b], 0)

    def p2_alloc():
        B = {}
        B['qe'] = [r3(A.bf(4 * 128), 128) for _ in range(2)]
        B['qo'] = [r3(A.bf(4 * 128), 128) for _ in range(2)]
        for i in range(2):
            P.op('pool', lambda e, i=i: e.memset(B['qe'][i][64:128, :, :], 0.0), writes=[('qblk', i)])
            P.op('pool', lambda e, i=i: e.memset(B['qo'][i][0:64, :, :], 0.0), writes=[('qblk', i)])
        B['cblk'] = [r3(A.bf(4 * 128), 128) for _ in range(2)]
        B['xres'] = [A.f32(D) for _ in range(2)]
        B['pbuf'] = [A.f32(514) for _ in range(3)]
        B['Cbuf'] = [A.f32(514) for _ in range(4)]
        B['wbuf'] = [A.bf(512) for _ in range(5)]
        B['wT'] = [r3(A.bf(512), 128) for _ in range(4)]
        B['osb'] = A.f32(512)
        B['onb'] = A.bf(512); B['sq'] = B['onb']; B['onT'] = r3(A.bf(512), 128)
        B['x2s'] = [A.f32(D)] * 2
        for i in range(3):
            P.op('pool', lambda e, i=i: e.memset(B['pbuf'][i][:, 512:513], 1.0), writes=[('pbuf', i)])
        P.op('dve', lambda e: e.memset(psf(6)[:, 0:16], 0.0), writes=['pszero'])
        return B

    class AttnPipe:
        def __init__(self, B):
            self.B = B
            self.items = []
            self.n = 0
            self.deferred = {}

        def add(self, item):
            self.items.append(item)

        def run(self):
            B = self.B
            n = len(self.items)
            for step in range(n + 12):
                if step < n:
                    self.s01(step)
                if 0 <= step - 1 < n:
                    self.s23(step - 1)
                if 0 <= step - 4 < n:
                    self.s45(step - 4)
                if 0 <= step - 6 < n:
                    self.s6(step - 6)
                for fn in self.deferred.pop(step, []):
                    fn()
            assert not self.deferred

        def defer(self, step, fn):
            self.deferred.setdefault(step, []).append(fn)

        def s01(self, i):
            it = self.items[i]
            if it.get('pre'):
                it['pre']()
            R, W = it['R'], it['W']
            zb = 1 + (i % 2)
            sl = i % 3
            qT = it['qT']; kTa = it['kT']
            rd = it['rd']
            if it['masked']:
                mw = it['mw']
                if W > mw:
                    P.op('pe', lambda e: e.matmul(psf(zb)[0:R, 0:W - mw], lhsT=qT, rhs=kTa[:, 0:W - mw], start=True, stop=True),
                         reads=rd, writes=[('ps', zb)])
                P.op('pe', lambda e: e.matmul(psf(zb)[0:R, W - mw:W], lhsT=qT, rhs=kTa[:, W - mw:W], start=True, stop=False),
                     reads=rd, writes=[('ps', zb)])
                P.op('pe', lambda e: e.matmul(psf(zb)[0:R, W - mw:W], lhsT=identB[0:R, 0:R], rhs=negB[0:R, 0:mw],
                                              start=False, stop=True),
                     reads=['identB', 'negB'], writes=[('ps', zb)])
            else:
                P.op('pe', lambda e: e.matmul(psf(zb)[0:R, 0:W], lhsT=qT, rhs=kTa, start=True, stop=True),
                     reads=rd, writes=[('ps', zb)])
            pb = B_ = self.B['pbuf'][sl]
            P.op('act', lambda e: e.activation(out=pb[0:R, 512 - W:512], in_=psf(zb)[0:R, 0:W], func=AF.Sigmoid, scale=-0.125),
                 reads=[('ps', zb)], writes=[('pbuf', sl)])

        def s23(self, i):
            it = self.items[i]
            R, W = it['R'], it['W']
            sl = i % 3
            s4 = i % 4
            s5 = i % 5
            pb = self.B['pbuf'][sl]; cb = self.B['Cbuf'][s4]; wb = self.B['wbuf'][s5]
            if it['first']:
                init = 1.0
                rds = [('pbuf', sl), 'pszero']
            else:
                pit = self.items[i - 1]
                psl = (i - 1) % 4
                init = self.B['Cbuf'][psl][0:R, 512 - pit['W']:513 - pit['W']]
                rds = [('pbuf', sl), 'pszero', ('Cbuf', psl)]
            P.op('dve', lambda e: e.tensor_tensor_scan(out=cb[0:R, 512 - W:513][:, ::-1], data0=pb[0:R, 512 - W:513][:, ::-1],
                                                       data1=psf(6)[0:R, 0:1].to_broadcast([R, W + 1]), initial=init,
                                                       op0=ALU.mult, op1=ALU.add),
                 reads=rds, writes=[('Cbuf', s4)])
            P.op('pool', lambda e: e.tensor_tensor(out=wb[0:R, 0:W], in0=cb[0:R, 513 - W:513], in1=cb[0:R, 512 - W:512],
                                                   op=ALU.subtract),
                 reads=[('Cbuf', s4)], writes=[('wbuf', s5)])

        def s45(self, i):
            it = self.items[i]
            R, W = it['R'], it['W']
            sl = i % 4
            s5 = i % 5
            wb = self.B['wbuf'][s5]; wT = self.B['wT'][sl]
            tb = 3 + (i % 2)
            tp = r3(psb(tb), 128)
            nkb = (W + 127) // 128
            for kb in range(nkb):
                kw = min(128, W - kb * 128)
                P.op('pe', lambda e, kb=kb, kw=kw: e.transpose(out=tp[0:kw, kb, 0:R], in_=wb[0:R, kb * 128:kb * 128 + kw],
                                                               identity=identB[0:R, 0:R]),
                     reads=[('wbuf', s5), 'identB'], writes=[('ps', tb)])
            kw0 = min(128, W)
            if True:
                P.op('act', lambda e: e.activation(out=wT[0:kw0, 0:nkb, 0:R], in_=tp[0:kw0, 0:nkb, 0:R], func=AF.Copy),
                     reads=[('ps', tb)], writes=[('wT', sl)])
            else:
                P.op('dve', lambda e: e.tensor_copy(out=wT[0:kw0, 0:nkb, 0:R], in_=tp[0:kw0, 0:nkb, 0:R]),
                     reads=[('ps', tb)], writes=[('wT', sl)])

        def s6(self, i):
            it = self.items[i]
            R, W = it['R'], it['W']
            sl = i % 4
            wT = self.B['wT'][sl]
            ob = it['obank']; h = it['h']
            nkb = (W + 127) // 128
            for kb in range(nkb):
                kw = min(128, W - kb * 128)
                vap = it['v'](kb)
                P.op('pe', lambda e, kb=kb, kw=kw, vap=vap: e.matmul(psf(ob)[0:R, h * HD:(h + 1) * HD], lhsT=wT[0:kw, kb, 0:R], rhs=vap,
                                                                     start=(it['first'] and kb == 0),
                                                                     stop=(it['last'] and kb == nkb - 1)),
                     reads=[('wT', sl)] + it['vrd'], writes=[('ps', ob)])
            if it.get('post'):
                it['post'](i + 6)

    def epilogue(pipe, B, R, ob, cT_fn, crd, xres_ap, xrd, dst_ap, uid):
        osb, sq, onb, onT = B['osb'], B['sq'], B['onb'], B['onT']
        x2s = B['x2s'][uid % 2]

        def e1():
            P.op('act', lambda e: e.activation(out=osb[0:R, :], in_=psf(ob)[0:R, :], func=AF.Copy),
                 reads=[('ps', ob)], writes=['osb'])
            P.op('dve', lambda e: e.tensor_tensor(out=sq[0:R, :], in0=osb[0:R, :], in1=osb[0:R, :], op=ALU.mult),
                 reads=['osb'], writes=['sq', 'onb'])
            P.op('dve', lambda e: e.tensor_reduce(out=sm[0:R, 8:16], in_=r3(sq, HD)[0:R], axis=AX.X, op=ALU.add),
                 reads=['sq', 'onb'], writes=['ss8'])
            rstd_pool(sm[0:R, 8:16], sm[0:R, 8:16], 1.0 / HD, 8, ['ss8'], ['ss8'])
            P.op('dve', lambda e: e.tensor_tensor(out=r3(onb, HD)[0:R], in0=r3(osb, HD)[0:R],
                                                  in1=sm[0:R, 8:16].unsqueeze(2).to_broadcast([R, NH, HD]), op=ALU.mult),
                 reads=['osb', 'ss8'], writes=['onb'])
            dump('osb', osb, [128, 512], ['osb'])
            dump('onb', onb, [128, 512], ['onb'], BF16)

        def e2():
            tp = r3(psb(7), 128)
            for j in range(4):
                P.op('pe', lambda e, j=j: e.transpose(out=tp[:, j, 0:R], in_=onb[0:R, j * 128:(j + 1) * 128],
                                                      identity=identB[0:R, 0:R]),
                     reads=['onb', 'identB'], writes=[('ps', 7)])
            P.op('act', lambda e: e.activation(out=onT[:, :, 0:R], in_=tp[:, 0:4, 0:R], func=AF.Copy),
                 reads=[('ps', 7)], writes=['onT'])
            for nh in range(2):
                bank = 3 + nh
                bank = 0 if nh == 0 else 7
                for j in range(8):
                    lhs = onT[:, j, 0:R] if j < 4 else cT_fn(j - 4)
                    P.op('pe', lambda e, j=j, lhs=lhs, bank=bank, nh=nh: e.matmul(
                        psf(bank)[0:R, :], lhsT=lhs, rhs=wout[:, j, nh * 512:(nh + 1) * 512], start=(j == 0), stop=(j == 7)),
                        reads=['onT', ('wout', j)] + crd, writes=[('ps', bank)])
                P.op('dve', lambda e, bank=bank, nh=nh: e.tensor_tensor(out=x2s[0:R, nh * 512:(nh + 1) * 512], in0=psf(bank)[0:R, :],
                                                                        in1=xres_ap[:, nh * 512:(nh + 1) * 512], op=ALU.add),
                     reads=[('ps', bank)] + xrd, writes=[('x2s', 0)])
            dump('onT', onT.rearrange("p a b -> p (a b)"), [128, 512], ['onT'], BF16)
            dump('x2s', x2s, [128, 1024], [('x2s', 0)])
            P.dma('sp', lambda e: e.dma_start(out=dst_ap, in_=x2s[0:R, :]), reads=[('x2s', 0)], key=('x2s', 0))
        return e1, e2

    def p2_prompt(B, b):
        pipe = AttnPipe(B)

        def loads(i):
            s = i % 2
            P.dma('sp', lambda e: e.dma_start(out=B['qe'][s][0:64, :, :],
                                              in_=qTd[b].rearrange("g p t -> p g t")[0:64, :, i * 128:(i + 1) * 128]),
                  writes=[('qblk', s)], key=('qblk', s))
            P.dma('sp', lambda e: e.dma_start(out=B['qo'][s][64:128, :, :],
                                              in_=qTd[b].rearrange("g p t -> p g t")[64:128, :, i * 128:(i + 1) * 128]),
                  writes=[('qblk', s)], key=('qblk', s))

        def loads_xc(i):
            s = i % 2
            P.dma('sp', lambda e: e.dma_start(out=B['cblk'][s], in_=cTd[b].rearrange("g p t -> p g t")[:, :, i * 128:(i + 1) * 128]),
                  writes=[('cblk', s)], key=('cblk', s))
            r0 = b * T + i * 128
            P.dma('sp', lambda e: e.dma_start(out=B['xres'][s], in_=xp[r0:r0 + 128, :]), writes=[('xres', s)], key=('xres', s))

        loads_xc(0)
        loads(0)
        for i in range(T // 128):
            s = i % 2
            ob = 5
            nk = (i + 1) * 128
            c_hi = (nk - 1) // 512
            for h in range(NH):
                hp, po = h // 2, (h % 2) * 64
                for ci, c in enumerate(range(c_hi, -1, -1)):
                    c0 = c * 512
                    W = min(512, nk - c0)
                    it = dict(R=128, W=W, h=h, obank=ob, first=(ci == 0), last=(c == 0), masked=(ci == 0), mw=128,
                              qT=(B['qe'] if h % 2 == 0 else B['qo'])[s][:, hp, :], kT=kT[:, hp, c0:c0 + W],
                              rd=[('qblk', s)] + [('kT', (c0 + x) // GT) for x in range(0, W, GT)],
                              v=(lambda kb, c0=c0, h=h: vv[:, c0 // 128 + kb, h * HD:(h + 1) * HD]),
                              vrd=[('vv', c0 // 128 + kb) for kb in range((W + 127) // 128)])
                    if h == 0 and ci == 0 and i + 1 < T // 128:
                        it['pre'] = (lambda i=i: loads(i + 1))
                    if h == NH - 1 and c == 0:
                        def post(step, i=i, s=s, ob=ob):
                            r0 = b * T + i * 128
                            e1, e2 = epilogue(pipe, B, 128, ob, lambda j: B['cblk'][s][:, j, :], [('cblk', s)],
                                              B['xres'][s], [('xres', s)], x2d[r0:r0 + 128, :], i)
                            e1()
                            if i + 1 < T // 128:
                                loads_xc(i + 1)
                            pipe.defer(step + 2, e2)
                        it['post'] = post
                    pipe.add(it)
        pipe.run()

    def p3():
        SB3 = 256
        wg = r3(A.bf(8 * DFF), DFF); wu = r3(A.bf(8 * DFF), DFF); wd = r3(A.bf(NF * D), D)
        fgb = A.f32(D)
        mark3 = A.top
        wst3 = [A.f32(DFF) for _ in range(4)]
        A.top = mark3
        for kc in range(8):
            load_cast(wg[:, kc, :], w_gate[kc * 128:(kc + 1) * 128, :], DFF, g2[:, kc:kc + 1], wst3, 'wst3')
            load_cast(wu[:, kc, :], w_up[kc * 128:(kc + 1) * 128, :], DFF, g2[:, kc:kc + 1], wst3, 'wst3')
        for f in range(NF):
            load_cast(wd[:, f, :], w_down[f * 128:(f + 1) * 128, :], D, None, wst3, 'wst3')
        P.dma('sp', lambda e: e.dma_start(out=fgb, in_=fg_d.partition_broadcast(128)), writes=['fgb'], key='k_fgb')
        P.barrier()
        x2b = [[A.f32(D) for _ in range(SB3 // 128)] for _ in range(2)]
        xn = [A.bf(D) for _ in range(2)]
        hfT = [r3(A.bf(8 * SB3), SB3) for _ in range(2)]
        act = r3(A.bf(NF * SB3), SB3)
        sg = [A.f32(SB3) for _ in range(2)]
        yf = [A.f32(D) for _ in range(2)]
        WG = []; WU = []

        blocks = []
        for i in range(2 * T // 128):
            blocks.append((i * 128, 128, yp[i * 128:(i + 1) * 128, :]))
        nb3 = SB3 // 128
        sbs = [blocks[i:i + nb3] for i in range(0, len(blocks), nb3)]
        sbs.append([(2 * T, 2 * ST, ys[:, :])])

        def cols_of(sbl):
            c = 0; out = []
            for (_, rows, _) in sbl:
                out.append(c); c += rows
            return out

        def norm_pre3(sbi):
            st_ = sbi % 2
            for j, (r0, rows, dst) in enumerate(sbs[sbi]):
                xb = x2b[st_][j]
                P.dma('pool', lambda e, xb=xb, r0=r0, rows=rows: e.dma_start(out=xb[0:rows, :], in_=x2d[r0:r0 + rows, :]),
                      writes=[('x2b', st_, j)], key=('x2b', st_, j))
                P.op('act', lambda e, xb=xb, rows=rows, j=j: e.activation(out=xn[j][0:rows, :], in_=xb[0:rows, :], func=AF.Square,
                                                                          accum_out=sm[0:rows, j:j + 1]),
                     reads=[('x2b', st_, j)], writes=[('xn', j), ('ss', j)])
                P.op('act', lambda e, rows=rows, j=j: e.activation(out=sm[0:rows, 2 + j:3 + j], in_=sm[0:rows, j:j + 1],
                                                                   func=AF.Sqrt, scale=1.0 / D, bias=epsT[0:rows, :]),
                     reads=[('ss', j), 'epsT'], writes=[('rs', j)])
                P.op('dve', lambda e, rows=rows, j=j: e.reciprocal(out=sm[0:rows, 2 + j:3 + j], in_=sm[0:rows, 2 + j:3 + j]),
                     reads=[('rs', j)], writes=[('rs', j)])
                P.op('dve', lambda e, xb=xb, rows=rows, j=j: e.tensor_scalar(out=xn[j][0:rows, :], in0=xb[0:rows, :],
                                                                             scalar1=sm[0:rows, 2 + j:3 + j], scalar2=None,
                                                                             op0=ALU.mult),
                     reads=[('x2b', st_, j), ('rs', j)], writes=[('xn', j)])

        def norm_post3(sbi):
            st_ = sbi % 2
            cols = cols_of(sbs[sbi])
            for j, (r0, rows, dst) in enumerate(sbs[sbi]):
                tp = r3(psb(0), 128)
                for kc in range(8):
                    P.op('pe', lambda e, kc=kc, rows=rows, j=j, tp=tp: e.transpose(out=tp[:, kc, 0:rows],
                                                                                  in_=xn[j][0:rows, kc * 128:(kc + 1) * 128],
                                                                                  identity=identB[0:rows, 0:rows]),
                         reads=[('xn', j), 'identB'], writes=[('ps', 0)])
                col = cols[j]
                P.op('act', lambda e, col=col, rows=rows, tp=tp: e.activation(out=hfT[st_][:, :, col:col + rows], in_=tp[:, :, 0:rows],
                                                                              func=AF.Copy),
                     reads=[('ps', 0)], writes=[('hfT', st_)])

        norm_pre3(0)
        norm_post3(0)
        for sbi, sbl in enumerate(sbs):
            st_ = sbi % 2
            n = sum(r for _, r, _ in sbl)
            cols = cols_of(sbl)
            hf = hfT[st_]
            if sbi + 1 < len(sbs):
                norm_pre3(sbi + 1)
            for f in range(NF):
                gb = 1 + 2 * (f % 2); ub_ = gb + 1
                for kc in range(8):
                    P.op('pe', lambda e, kc=kc, f=f, gb=gb, n=n, hf=hf: e.matmul(psf(gb)[:, 0:n], lhsT=wg[:, kc, f * 128:(f + 1) * 128],
                                                                                 rhs=hf[:, kc, 0:n], start=(kc == 0), stop=(kc == 7)),
                         reads=[('hfT', st_)] + WG, writes=[('ps', gb)])
                for kc in range(8):
                    P.op('pe', lambda e, kc=kc, f=f, ub_=ub_, n=n, hf=hf: e.matmul(psf(ub_)[:, 0:n], lhsT=wu[:, kc, f * 128:(f + 1) * 128],
                                                                                   rhs=hf[:, kc, 0:n], start=(kc == 0), stop=(kc == 7)),
                         reads=[('hfT', st_)] + WU, writes=[('ps', ub_)])
                P.op('act', lambda e, f=f, gb=gb, n=n: e.activation(out=sg[f % 2][:, 0:n], in_=psf(gb)[:, 0:n], func=AF.Silu),
                     reads=[('ps', gb)], writes=[('sg', f % 2)])
                P.op('dve', lambda e, f=f, ub_=ub_, n=n: e.tensor_tensor(out=act[:, f, 0:n], in0=psf(ub_)[:, 0:n], in1=sg[f % 2][:, 0:n],
                                                                         op=ALU.mult),
                     reads=[('ps', ub_), ('sg', f % 2)], writes=['act'])
            if sbi + 1 < len(sbs):
                norm_post3(sbi + 1)
            for j, (r0, rows, dst) in enumerate(sbl):
                xb = x2b[st_][j]; yo = yf[j % 2]; c0 = cols[j]
                for nh in range(2):
                    bank = 5 + nh
                    for f in range(NF):
                        P.op('pe', lambda e, f=f, nh=nh, bank=bank, rows=rows, c0=c0: e.matmul(
                            psf(bank)[0:rows, :], lhsT=act[:, f, c0:c0 + rows], rhs=wd[:, f, nh * 512:(nh + 1) * 512],
                            start=(f == 0), stop=(f == NF - 1)), reads=['act'], writes=[('ps', bank)])
                    P.op('dve', lambda e, nh=nh, bank=bank, rows=rows, xb=xb, yo=yo: e.tensor_tensor(
                        out=yo[0:rows, nh * 512:(nh + 1) * 512], in0=psf(bank)[0:rows, :], in1=xb[0:rows, nh * 512:(nh + 1) * 512],
                        op=ALU.add), reads=[('ps', bank), ('x2b', st_, j)], writes=[('yf', j % 2)])
                sl = 4 + (j % 2)
                P.op('act', lambda e, rows=rows, yo=yo, sl=sl, xb=xb: e.activation(out=xb[0:rows, :], in_=yo[0:rows, :],
                                                                                 func=AF.Square, accum_out=sm[0:rows, sl:sl + 1]),
                     reads=[('yf', j % 2)], writes=[('x2b', st_, j), ('ss', sl)])
                P.op('act', lambda e, rows=rows, sl=sl: e.activation(out=sm[0:rows, sl + 2:sl + 3], in_=sm[0:rows, sl:sl + 1],
                                                                     func=AF.Sqrt, scale=1.0 / D, bias=epsT[0:rows, :]),
                     reads=[('ss', sl), 'epsT'], writes=[('rs', sl)])
                P.op('dve', lambda e, rows=rows, sl=sl: e.reciprocal(out=sm[0:rows, sl + 2:sl + 3], in_=sm[0:rows, sl + 2:sl + 3]),
                     reads=[('rs', sl)], writes=[('rs', sl)])
                P.op('dve', lambda e, rows=rows, yo=yo, sl=sl: e.scalar_tensor_tensor(
                    out=yo[0:rows, :], in0=yo[0:rows, :], scalar=sm[0:rows, sl + 2:sl + 3], in1=fgb[0:rows, :],
                    op0=ALU.mult, op1=ALU.mult), reads=[('yf', j % 2), ('rs', sl), 'fgb'], writes=[('yf', j % 2)])
                P.dma('sp', lambda e, rows=rows, yo=yo, dst=dst: e.dma_start(out=dst, in_=yo[0:rows, :]),
                      reads=[('yf', j % 2)], key=('yf', j % 2))

    def p_sample(B1, S):
        n = 2 * ST
        B = B1
        norm_block(B, 0, xs[:, :], n, 0, 0, 'x')
        hT = B['hT'][0]
        for fg in range(4):
            bank = gbank()
            mm_group(bank, n, lambda kc: win[:, kc, fg * 128:(fg + 1) * 128], lambda kc: hT[:, kc, 0:n], 8, [('hT', 0)] + WIN_ALL)
            evac(S['qTe'][0:64, fg, :], psf(bank)[0:64, 0:n], [('ps', bank)], ['qTn'])
            evac(S['qTo'][64:128, fg, :], psf(bank)[64:128, 0:n], [('ps', bank)], ['qTn'])
        for fg in range(4):
            bank = gbank()
            mm_group(bank, n, lambda kc: win[:, kc, 512 + fg * 128:512 + (fg + 1) * 128], lambda kc: hT[:, kc, 0:n], 8,
                     [('hT', 0)] + WIN_ALL)
            evac(S['kTn'][:, fg, :], psf(bank)[:, 0:n], [('ps', bank)], ['kTn'])
        bank = gbank()
        mm_group(bank, 512, lambda kc: hT[:, kc, 0:n], lambda kc: win[:, kc, 512:1024], 8, [('hT', 0)] + WIN_ALL, m=n)
        evac(B['kst'][0][0:n, :], psf(bank)[0:n, :], [('ps', bank)], [('kst', 0)])
        for b in range(2):
            P.dma('sp', lambda e, b=b: e.dma_start(out=ks[b].rearrange("h t d -> t h d"),
                                                   in_=r3(B['kst'][0], HD)[b * ST:(b + 1) * ST]),
                  reads=[('kst', 0)], key=('kst', 0))
        for b in range(2):
            bank = gbank()
            mm_group(bank, 512, lambda kc: hT[:, kc, b * ST:(b + 1) * ST], lambda kc: win[:, kc, 1024:1536], 8,
                     [('hT', 0)] + WIN_ALL, m=ST)
            evac(B['vst'][b][0:ST, :], psf(bank)[0:ST, :], [('ps', bank)], [('vst', b)])
            P.op('pool', lambda e, b=b: e.tensor_copy(out=S['vn'][b][0:ST, :], in_=B['vst'][b][0:ST, :]),
                 reads=[('vst', b)], writes=[('vn', b)])
            P.dma('sp', lambda e, b=b: e.dma_start(out=vs[b].rearrange("h t d -> t h d"), in_=r3(B['vst'][b], HD)[0:ST]),
                  reads=[('vst', b)], key=('vst', b))
        P.dma('sp', lambda e: e.dma_start(out=S['sct'][0:2 * CH, :], in_=sc[:, :]), writes=['sct'], key='sct')
        ub = B['ub'][0]
        seg_w = CH + ST
        for cg in range(4):
            P.op('pe', lambda e, cg=cg: e.transpose(out=psf(0)[:, 0:2 * CH], in_=S['sct'][0:2 * CH, cg * 128:(cg + 1) * 128],
                                                    identity=identF[0:2 * CH, 0:2 * CH]),
                 reads=['sct', 'identF'], writes=[('ps', 0)])
            for b in range(2):
                P.op('act', lambda e, cg=cg, b=b: e.activation(out=ub[:, cg, b * seg_w:b * seg_w + CH],
                                                               in_=psf(0)[:, b * CH:(b + 1) * CH], func=AF.Copy),
                     reads=[('ps', 0)], writes=[('ubh', 0)])
        p1_feature(B, 0, n, 0, [(0, ST, 0), (ST, ST, seg_w)], [B['uf32'], S['uf32b']])
        for cg in range(4):
            P.op('pool', lambda e, cg=cg: e.tensor_copy(out=S['cTn'][:, cg, :], in_=B['cTs'][:, cg, 0:n]),
                 reads=['cTs'], writes=['cTn'])
        conv_out(B, cs[0], 0, B['uf32'])
        conv_out(B, cs[1], ST, S['uf32b'])

    GB = 8

    def cache_issue(b, g, vq='pool'):
        stg = [r3(A.t[:, win_off // 2 + i * GB * 1024: win_off // 2 + (i + 1) * GB * 1024].bitcast(F32), 512) for i in range(4)]
        ks_ = stg[g % 2]; vs_ = stg[2 + g % 2]
        for h in range(NH):
            P.dma('sp', lambda e, h=h, g=g, ks_=ks_: e.dma_start(
                out=ks_[:, :, h * HD:(h + 1) * HD],
                in_=ck[b, h, g * GB * 128:(g + 1) * GB * 128, :].rearrange("(k p) d -> p k d", p=128)),
                writes=[('kstg', g % 2)], key=('kstg', g % 2))
        for h in range(NH):
            P.dma(vq, lambda e, h=h, g=g, vs_=vs_: e.dma_start(
                out=vs_[:, :, h * HD:(h + 1) * HD],
                in_=cv[b, h, g * GB * 128:(g + 1) * GB * 128, :].rearrange("(k p) d -> p k d", p=128)),
                writes=[('vstg', g % 2)], key=('vstg', g % 2))

    def cache_proc(g):
        stg = [r3(A.t[:, win_off // 2 + i * GB * 1024: win_off // 2 + (i + 1) * GB * 1024].bitcast(F32), 512) for i in range(4)]
        ks_ = stg[g % 2]; vs_ = stg[2 + g % 2]
        for j in range(GB):
            blk = g * GB + j
            tb = 3 + (blk % 2)
            for hp in range(4):
                P.op('pe', lambda e, j=j, hp=hp, tb=tb, ks_=ks_: e.transpose(out=psf(tb)[:, hp * 128:(hp + 1) * 128],
                                                                         in_=ks_[:, j, hp * 128:(hp + 1) * 128], identity=identF),
                     reads=[('kstg', g % 2), 'identF'], writes=[('ps', tb)])
            evac(kT[:, :, blk * 128:(blk + 1) * 128], r3(psf(tb), 128), [('ps', tb)], ['kTc'])
            evac(vv[:, blk, :], vs_[:, j, :], [('vstg', g % 2)], ['vvc'])

    def sample_attn(S, B, b, prefetched):
        for g in range(32 // GB):
            if g >= prefetched:
                cache_issue(b, g)
            if g + 1 < 32 // GB and g + 1 >= prefetched and False:
                pass
            cache_proc(g)
        P.dma('sp', lambda e: e.dma_start(out=B['xres'][b][0:ST, :], in_=xs[b * ST:(b + 1) * ST, :]),
              writes=[('xres', b)], key=('xres', b))
        P.barrier()
        if b == 0:
            cache_issue(1, 0, 'sp')
            cache_issue(1, 1, 'sp')
        pipe = AttnPipe(B)
        ob = 5
        for h in range(NH):
            hp, po = h // 2, (h % 2) * 64
            qTa = (S['qTe'] if h % 2 == 0 else S['qTo'])[:, hp, b * ST:(b + 1) * ST]
            it = dict(R=ST, W=ST, h=h, obank=ob, first=True, last=False, masked=True, mw=ST,
                      qT=qTa, kT=S['kTn'][:, hp, b * ST:(b + 1) * ST], rd=[],
                      v=(lambda kb, h=h: S['vn'][b][0:ST, h * HD:(h + 1) * HD]), vrd=[])
            pipe.add(it)
            for c in range(7, -1, -1):
                c0 = c * 512
                it = dict(R=ST, W=512, h=h, obank=ob, first=False, last=(c == 0), masked=False, mw=0,
                          qT=qTa, kT=kT[:, hp, c0:c0 + 512], rd=[],
                          v=(lambda kb, c0=c0, h=h: vv[:, c0 // 128 + kb, h * HD:(h + 1) * HD]), vrd=[])
                if h == NH - 1 and c == 0:
                    def post(step):
                        r0 = 2 * T + b * ST
                        e1, e2 = epilogue(pipe, B, ST, ob, lambda j: S['cTn'][:, j, b * ST:(b + 1) * ST], [],
                                          B['xres'][b][0:ST, :], [('xres', b)], x2d[r0:r0 + ST, :], b)
                        e1()
                        pipe.defer(step + 2, e2)
                    it['post'] = post
                pipe.add(it)
        pipe.run()
        P.barrier()

    for b in range(2):
        A.top = phase_top
        B1 = p1_alloc()
        p1_prompt(B1, b)
        P.barrier()
        A.top = phase_top
        B2 = p2_alloc()
        p2_prompt(B2, b)
        P.barrier()
    A.top = phase_top
    S = {}
    S['kTn'] = r3(A.bf(4 * 64), 64)
    S['qTe'] = r3(A.bf(4 * 64), 64)
    S['qTo'] = r3(A.bf(4 * 64), 64)
    P.op('pool', lambda e: e.memset(S['qTe'][64:128, :, :], 0.0), writes=['qTn'])
    P.op('pool', lambda e: e.memset(S['qTo'][0:64, :, :], 0.0), writes=['qTn'])
    S['cTn'] = r3(A.bf(4 * 64), 64)
    S['vn'] = [A.bf(512) for _ in range(2)]
    S['sct'] = A.f32(512)
    S['uf32b'] = r3(A.f32(4 * 32), 32)
    s_top = A.top
    B1 = p1_alloc(1)
    p_sample(B1, S)
    P.barrier()
    A.top = s_top
    B2 = p2_alloc()
    sample_attn(S, B2, 0, 0)
    sample_attn(S, B2, 1, 2)
    A.top = base_top
    p3()
    P.barrier()

    with nc.Block() as block:
        P.emit(block)
    stack.close()
    return nc


_CACHE = {}


def _consts():
    c = np.zeros((128, 384), np.float32)
    c[:, 0:128] = np.eye(128, dtype=np.float32)
    i = np.arange(128)
    c[:, 128:256] = np.where(i[None, :] >= i[:, None], NEG, 0.0).astype(np.float32)
    c[:, 256:384] = 1.0
    return c


def kernel(x_prompt, x_sample, cache_k, cache_v, state_conv, w_in, sb_norm_g, conv_w, conv_b, conv_ln_g,
           conv_ln_b, w_out, norm1_g, norm2_g, w_gate, w_up, w_down, final_g, _ncores=8):
    f = lambda a: np.ascontiguousarray(np.asarray(a, dtype=np.float32))
    nc = bass.Bass("TRN2", target_bir_lowering=False)
    build_program(nc)
    cst = _consts()
    shared = dict(w_in=f(w_in[0]), sbg=f(sb_norm_g[0]).reshape(512), convw=f(conv_w[0]), convb=f(conv_b[0]),
                  lng=f(conv_ln_g[0]), lnb=f(conv_ln_b[0]), w_out=f(w_out[0]), g1=f(norm1_g[0]), g2=f(norm2_g[0]),
                  w_gate=f(w_gate[0]), w_up=f(w_up[0]), w_down=f(w_down[0]), fg=f(final_g), cst=cst)
    in_maps = []
    for c in range(_ncores):
        m = dict(shared)
        m['xp'] = f(x_prompt[2 * c:2 * c + 2]).reshape(2 * T, D)
        m['xs'] = f(x_sample[2 * c:2 * c + 2]).reshape(2 * ST, D)
        m['ck'] = f(cache_k[0, 2 * c:2 * c + 2])
        m['cv'] = f(cache_v[0, 2 * c:2 * c + 2])
        m['sc'] = f(state_conv[0, 2 * c:2 * c + 2]).reshape(2 * CH, 512)
        in_maps.append(m)
    res = run_bass_kernel_spmd(nc, in_maps, core_ids=list(range(_ncores)))
    R = res.results
    if DEBUG:
        DBG['r0'] = R[0]
    cat = lambda k, shp: np.concatenate([np.asarray(r[k]).reshape(shp) for r in R], axis=0)
    y_prompt = cat('yp', (2, T, D))
    y_sample = cat('ys', (2, ST, D))
    k_prompt = cat('kp', (2, NH, T, HD))[None]
    v_prompt = cat('vp', (2, NH, T, HD))[None]
    conv_prompt = cat('cp', (2, CH, 512))[None]
    k_sample = cat('ks', (2, NH, ST, HD))[None]
    v_sample = cat('vs', (2, NH, ST, HD))[None]
    conv_sample = cat('cs', (2, CH, 512))[None]
    return (y_prompt, y_sample, k_prompt, v_prompt, conv_prompt, k_sample, v_sample, conv_sample)
```
